# Optimizing a Trainium2 kernel written in Bass

```python
import jax, jax.numpy as jnp
from jax import lax
import numpy as np

D_MODEL = 2048
BATCH = 4
SEQ = 4096
DEPTH = 1

HEAD_DIM = D_MODEL // 16
FOX_HEADS = 6
SWA_HEADS = 6
SWA_KV_HEADS = 2
MEM_HEADS = 4
MEM_LEN = 256
WINDOW = 128
Q_BLOCK = 128
D_FF = 5632
EPS = 1e-6
NEG_INF = -1e30

FOX_W = FOX_HEADS * HEAD_DIM
SWA_Q_W = SWA_HEADS * HEAD_DIM
SWA_KV_W = SWA_KV_HEADS * HEAD_DIM
MEM_W = MEM_HEADS * HEAD_DIM
MIX_WIDTH = FOX_W + SWA_Q_W + MEM_W
IN_SPLITS = [FOX_W, FOX_W, FOX_W, FOX_HEADS, SWA_Q_W, SWA_KV_W, SWA_KV_W, MEM_W]
IN_WIDTH = int(sum(IN_SPLITS))
IN_CUTS = [int(c) for c in np.cumsum(IN_SPLITS)[:-1]]

kernel_name = "hybrid_fox_swa_memory_macaron"


def rms_norm(x, g):
    xf = x.astype(jnp.float32)
    y = xf * lax.rsqrt(jnp.mean(xf * xf, axis=-1, keepdims=True) + EPS)
    return (y * g.astype(jnp.float32)).astype(x.dtype)


def swiglu(x, w_gate, w_up, w_down):
    return (jax.nn.silu(x @ w_gate) * (x @ w_up)) @ w_down


def alibi_slopes(n):
    return jnp.asarray(2.0 ** (-8.0 * np.arange(1, n + 1) / n), dtype=jnp.float32)


def forgetting_attention(q, k, v, log_f):
    B, S, H, D = q.shape
    nb = S // Q_BLOCK
    scale = D ** -0.5
    c = jnp.cumsum(log_f, axis=1)
    c_k = c.transpose(0, 2, 1)
    kf = k.astype(jnp.float32)
    vf = v.astype(jnp.float32)
    qb = q.astype(jnp.float32).reshape(B, nb, Q_BLOCK, H, D).transpose(1, 0, 2, 3, 4)
    cb = c.reshape(B, nb, Q_BLOCK, H).transpose(1, 0, 3, 2)
    kpos = jnp.arange(S)

    def block(args):
        qi, ci, i = args
        qpos = i * Q_BLOCK + jnp.arange(Q_BLOCK)
        s = jnp.einsum('bqhd,bkhd->bhqk', qi, kf) * scale
        s = s + ci[..., None] - c_k[:, :, None, :]
        causal = kpos[None, :] <= qpos[:, None]
        s = jnp.where(causal, s, NEG_INF)
        p = jax.nn.softmax(s, axis=-1)
        return jnp.einsum('bhqk,bkhd->bqhd', p, vf)

    out = lax.map(block, (qb, cb, jnp.arange(nb)))
    return out.transpose(1, 0, 2, 3, 4).reshape(B, S, H, D).astype(q.dtype)


def sliding_window_sink_attention(q, k, v, sinks, slopes):
    B, S, Hq, D = q.shape
    Hkv = k.shape[2]
    G = Hq // Hkv
    nb = S // WINDOW
    scale = D ** -0.5
    qb = q.astype(jnp.float32).reshape(B, nb, WINDOW, Hkv, G, D)

    def band(t):
        tb = t.astype(jnp.float32).reshape(B, nb, WINDOW, Hkv, D)
        prev = jnp.pad(tb[:, :-1], ((0, 0), (1, 0), (0, 0), (0, 0), (0, 0)))
        return jnp.concatenate([prev, tb], axis=2)

    kb, vb = band(k), band(v)
    s = jnp.einsum('bnqhgd,bnkhd->bnhgqk', qb, kb) * scale
    r = jnp.arange(WINDOW)[:, None]
    j = jnp.arange(2 * WINDOW)[None, :]
    dist = WINDOW + r - j
    in_window = (dist >= 0) & (dist < WINDOW)
    valid = in_window[None] & ((jnp.arange(nb)[:, None, None] > 0) | (j[None] >= WINDOW))
    alibi = -slopes.astype(jnp.float32).reshape(Hkv, G)[:, :, None, None] * dist.astype(jnp.float32)
    s = s + alibi[None, None]
    s = jnp.where(valid[None, :, None, None], s, NEG_INF)
    sink = jnp.broadcast_to(
        sinks.astype(jnp.float32).reshape(Hkv, G)[None, None, :, :, None, None],
        s.shape[:-1] + (1,))
    p = jax.nn.softmax(jnp.concatenate([s, sink], axis=-1), axis=-1)[..., :-1]
    out = jnp.einsum('bnhgqk,bnkhd->bnqhgd', p, vb)
    return out.reshape(B, S, Hq, D).astype(q.dtype)


def memory_attention(q, mk, mv):
    scale = q.shape[-1] ** -0.5
    s = jnp.einsum('bqhd,bmhd->bhqm', q.astype(jnp.float32), mk.astype(jnp.float32)) * scale
    p = jax.nn.softmax(s, axis=-1)
    return jnp.einsum('bhqm,bmhd->bqhd', p, mv.astype(jnp.float32)).astype(q.dtype)


def setup_inputs(seed: int = 0) -> dict:
    key = jax.random.key(seed)
    ks = jax.random.split(key, 32)
    f32 = jnp.float32

    def w(k, shape, fan_in):
        return jax.random.normal(k, shape, f32) * (fan_in ** -0.5)

    def gain(k, shape):
        return 1.0 + 0.02 * jax.random.normal(k, shape, f32)

    L = DEPTH
    return {
        "x": jax.random.normal(ks[0], (BATCH, SEQ, D_MODEL), f32),
        "mem": jax.random.normal(ks[1], (BATCH, MEM_LEN, D_MODEL), f32),
        "ffn1_norm": gain(ks[2], (L, D_MODEL)),
        "ffn1_gate": w(ks[3], (L, D_MODEL, D_FF), D_MODEL),
        "ffn1_up": w(ks[4], (L, D_MODEL, D_FF), D_MODEL),
        "ffn1_down": w(ks[5], (L, D_FF, D_MODEL), D_FF),
        "mix_norm": gain(ks[6], (L, D_MODEL)),
        "mem_norm": gain(ks[7], (L, D_MODEL)),
        "w_in": w(ks[8], (L, D_MODEL, IN_WIDTH), D_MODEL),
        "forget_bias": jax.random.uniform(ks[9], (L, FOX_HEADS), f32, 1.0, 4.0),
        "w_mem_k": w(ks[10], (L, D_MODEL, MEM_W), D_MODEL),
        "w_mem_v": w(ks[11], (L, D_MODEL, MEM_W), D_MODEL),
        "fox_q_gain": gain(ks[12], (L, HEAD_DIM)),
        "fox_k_gain": gain(ks[13], (L, HEAD_DIM)),
        "swa_q_gain": gain(ks[14], (L, HEAD_DIM)),
        "swa_k_gain": gain(ks[15], (L, HEAD_DIM)),
        "swa_sinks": jax.random.normal(ks[16], (L, SWA_HEADS), f32),
        "mem_q_gain": gain(ks[17], (L, HEAD_DIM)),
        "mem_k_gain": gain(ks[18], (L, HEAD_DIM)),
        "w_out": w(ks[19], (L, MIX_WIDTH, D_MODEL), MIX_WIDTH),
        "ffn2_norm": gain(ks[20], (L, D_MODEL)),
        "ffn2_gate": w(ks[21], (L, D_MODEL, D_FF), D_MODEL),
        "ffn2_up": w(ks[22], (L, D_MODEL, D_FF), D_MODEL),
        "ffn2_down": w(ks[23], (L, D_FF, D_MODEL), D_FF),
    }


def reference(x, mem, ffn1_norm, ffn1_gate, ffn1_up, ffn1_down, mix_norm, mem_norm, w_in,
              forget_bias, w_mem_k, w_mem_v, fox_q_gain, fox_k_gain, swa_q_gain, swa_k_gain,
              swa_sinks, mem_q_gain, mem_k_gain, w_out, ffn2_norm, ffn2_gate, ffn2_up, ffn2_down):
    B, S, _ = x.shape
    M = mem.shape[1]
    slopes = alibi_slopes(SWA_HEADS).astype(x.dtype)
    for l in range(DEPTH):
        x = x + 0.5 * swiglu(rms_norm(x, ffn1_norm[l]), ffn1_gate[l], ffn1_up[l], ffn1_down[l])

        h = rms_norm(x, mix_norm[l])
        proj = h @ w_in[l]
        fq, fk, fv, f_logit, sq, sk, sv, mq = jnp.split(proj, IN_CUTS, axis=-1)

        fq = rms_norm(fq.reshape(B, S, FOX_HEADS, HEAD_DIM), fox_q_gain[l])
        fk = rms_norm(fk.reshape(B, S, FOX_HEADS, HEAD_DIM), fox_k_gain[l])
        fv = fv.reshape(B, S, FOX_HEADS, HEAD_DIM)
        log_f = jax.nn.log_sigmoid(f_logit.astype(jnp.float32) + forget_bias[l].astype(jnp.float32))
        out_a = forgetting_attention(fq, fk, fv, log_f)

        sq = rms_norm(sq.reshape(B, S, SWA_HEADS, HEAD_DIM), swa_q_gain[l])
        sk = rms_norm(sk.reshape(B, S, SWA_KV_HEADS, HEAD_DIM), swa_k_gain[l])
        sv = sv.reshape(B, S, SWA_KV_HEADS, HEAD_DIM)
        out_b = sliding_window_sink_attention(sq, sk, sv, swa_sinks[l], slopes)

        mn = rms_norm(mem, mem_norm[l])
        mk = rms_norm((mn @ w_mem_k[l]).reshape(B, M, MEM_HEADS, HEAD_DIM), mem_k_gain[l])
        mv = (mn @ w_mem_v[l]).reshape(B, M, MEM_HEADS, HEAD_DIM)
        mq = rms_norm(mq.reshape(B, S, MEM_HEADS, HEAD_DIM), mem_q_gain[l])
        out_c = memory_attention(mq, mk, mv)

        mixed = jnp.concatenate([out_a.reshape(B, S, FOX_W), out_b.reshape(B, S, SWA_Q_W),
                                 out_c.reshape(B, S, MEM_W)], axis=-1)
        x = x + mixed @ w_out[l]

        x = x + 0.5 * swiglu(rms_norm(x, ffn2_norm[l]), ffn2_gate[l], ffn2_up[l], ffn2_down[l])
    return x
```

```python
import numpy as np
from contextlib import ExitStack
import concourse.bass as bass
import concourse.mybir as mybir
from concourse.bass_utils import run_bass_kernel_spmd

F32 = mybir.dt.float32
BF16 = mybir.dt.bfloat16
AF = mybir.ActivationFunctionType
ALU = mybir.AluOpType
AX = mybir.AxisListType

D = 2048
DFF = 5632
NFB = DFF // 128
NCH = D // 128
SEQ = 4096
NTOK = 2048
NTB = 16
TILE_TB = 8
NTILE = NTB // TILE_TB
TT = TILE_TB * 128
HD = 128
EPS = 1e-6
NEG = -30000.0
SCALE = HD ** -0.5

EPOCH = 8000
COMPUTE = ("tensor", "vector", "scalar", "gpsimd")
ENGINES = ("tensor", "vector", "scalar", "gpsimd", "sync")


class Buf:
    __slots__ = ("name", "w", "r", "dsem")

    def __init__(self, name):
        self.name = name
        self.w = None
        self.r = []
        self.dsem = None


class DmaSem:
    def __init__(self, handle):
        self.h = handle
        self.count = 0


class Op:
    __slots__ = ("eng", "fn", "deps", "inc", "dma", "dsem", "tok", "ninc")

    def __init__(self, eng, fn, dma=False, dsem=None, ninc=16):
        self.eng = eng
        self.fn = fn
        self.deps = []
        self.inc = False
        self.dma = dma
        self.dsem = dsem
        self.tok = None
        self.ninc = ninc


class Prog:
    def __init__(self, nc, stack):
        self.nc = nc
        self.stack = stack
        self.seq = []
        self.ecount = {e: 0 for e in COMPUTE}
        self.esems = {e: [] for e in COMPUTE}
        self.waited = {e: {} for e in ENGINES}
        self.last = {e: None for e in ENGINES}
        self.fence_deps = {e: [] for e in ENGINES}
        self.dma_last = {}
        self.bufs = []
        self.nsem = 0
        self.nops = 0
        self.shared_dsems = []

    def buf(self, name):
        b = Buf(name)
        self.bufs.append(b)
        return b

    def bufs_n(self, name, n):
        return [self.buf(f"{name}{i}") for i in range(n)]

    def sem(self, name):
        self.nsem += 1
        return self.stack.enter_context(self.nc.semaphore(f"{name}_{self.nsem}"))

    def dsem_for(self, buf, kind):
        if buf.dsem is None:
            buf.dsem = {}
        if kind not in buf.dsem:
            free = self.__dict__.setdefault("free_dsems", {}).setdefault(kind, [])
            buf.dsem[kind] = free.pop() if free else DmaSem(self.sem("d"))
        return buf.dsem[kind]

    def _add_dep(self, op, dep):
        if dep is None or dep is op:
            return
        if dep.tok is not None and not dep.inc and not dep.dma:
            raise RuntimeError("dependency on already-replayed non-inc op")
        if (not dep.dma) and (not op.dma) and dep.eng == op.eng and op.eng == "tensor":
            return
        dep.inc = True
        op.deps.append(dep)

    def op(self, eng, fn, reads=(), writes=(), dma=False, dbuf=None, ninc=16):
        if getattr(self, "stopped", False):
            return None
        dsem = None
        if dma:
            dsem = self.dsem_for(dbuf, eng + ("_cc" if ninc == 1 else ""))
        o = Op(eng, fn, dma=dma, dsem=dsem, ninc=ninc)
        if dma:
            o.inc = True
            self.dma_last[id(dsem)] = o
        for d in self.fence_deps[eng]:
            self._add_dep(o, d)
        self.fence_deps[eng] = []
        for b in reads:
            self._add_dep(o, b.w)
        for b in writes:
            self._add_dep(o, b.w)
            for r in b.r:
                self._add_dep(o, r)
        for b in writes:
            b.w = o
            b.r = []
        for b in reads:
            if b.w is o:
                continue
            if not dma:
                b.r = [r for r in b.r if r.dma or r.eng != eng]
            b.r.append(o)
        self.seq.append(o)
        self.last[eng] = o
        self.nops += 1
        return o

    def fence(self):
        deps = [self.last[e] for e in COMPUTE if self.last[e] is not None]
        deps += list(self.dma_last.values())
        self.dma_last = {}
        for d in deps:
            d.inc = True
        for e in ENGINES:
            self.fence_deps[e] = list(deps)
        free = self.__dict__.setdefault("free_dsems", {})
        for b in self.bufs:
            b.w = None
            b.r = []
            if b.dsem is not None:
                for kind, ds in b.dsem.items():
                    free.setdefault(kind, []).append(ds)
                b.dsem = None

    def _esem(self, eng, idx):
        while len(self.esems[eng]) <= idx:
            self.esems[eng].append(self.sem("e"))
        return self.esems[eng][idx]

    def replay(self):
        nc = self.nc
        prog = self
        per = {e: [] for e in ENGINES}
        for o in self.seq:
            if o.dma:
                o.dsem.count += o.ninc
                o.tok = (o.dsem.h, o.dsem.count)
            elif o.inc:
                c = self.ecount[o.eng]
                h = self._esem(o.eng, c // EPOCH)
                o.tok = (h, c % EPOCH + 1)
                self.ecount[o.eng] = c + 1
            else:
                o.tok = (None, 0)
            per[o.eng].append(o)
        self.seq = []

        def run(eng_name, eng):
            waited = prog.waited[eng_name]
            for o in per[eng_name]:
                need = {}
                for d in o.deps:
                    h, v = d.tok
                    k = id(h)
                    if waited.get(k, 0) >= v:
                        continue
                    if k not in need or need[k][1] < v:
                        need[k] = (h, v)
                for k, (h, v) in need.items():
                    eng.wait_ge(h, v)
                    waited[k] = v
                ins = o.fn(eng)
                if o.dma or o.inc:
                    ins.then_inc(o.tok[0], o.ninc if o.dma else 1)

        with nc.Block() as block:
            @block.tensor
            def _(e):
                run("tensor", e)

            @block.vector
            def _(e):
                run("vector", e)

            @block.scalar
            def _(e):
                run("scalar", e)

            @block.gpsimd
            def _(e):
                run("gpsimd", e)

            @block.sync
            def _(e):
                run("sync", e)

    def end_phase(self):
        self.fence()
        if not getattr(self, "single_block", False):
            self.replay()
        self.nphase = getattr(self, "nphase", 0) + 1
        if getattr(self, "stop", None) is not None and self.nphase == self.stop:
            self.stopped = True


class Ctx:
    pass


def build_program(stop=None):
    nc = bass.Bass("TRN2", target_bir_lowering=False)
    K = Ctx()
    K.nc = nc

    def din(name, shape, dt=F32):
        return nc.dram_tensor(name, list(shape), dt, kind="ExternalInput").ap()

    x_d = din("x_loc", [NTOK, D])
    mem_d = din("mem_b", [256, D])
    grow_d = din("grows", [4, 128, D])
    hg_d = din("hgains", [128, 6])
    fb_d = din("fbias", [128, TILE_TB * 6])
    sinks_d = din("sinks", [128, 6])
    hsel_d = din("hsel", [128, 1])
    ident_d = din("ident", [128, 128])
    tri_d = din("tri", [128, 128])
    ones_d = din("ones", [128, 128])
    fmask_d = din("fmask", [128, 2 * 128])
    sbias_d = din("sbias", [128, 18 * 128])
    wgu_d = [din(f"wgu{i}", [NFB, 128, 2 * NCH * 128]) for i in (1, 2)]
    wd_d = [din(f"wd{i}", [NCH, 128, NFB * 128]) for i in (1, 2)]
    win_d = din("win", [8, 128, NCH * 512])
    wfl_d = din("wfl", [128, NCH * 6])
    wmem_d = din("wmem", [2, 128, NCH * 512])
    wout_d = din("wout", [4, 128, NCH * 512])
    out_d = nc.dram_tensor("out_loc", [NTOK, D], F32, kind="ExternalOutput").ap()

    x1_d = nc.dram_tensor("x1_s", [NTOK, D], F32).ap()
    x2_d = nc.dram_tensor("x2_s", [NTOK, D], F32).ap()
    qT_d = nc.dram_tensor("qT_s", [16 * 128, NTOK], BF16).ap()
    kv_in_t = [nc.dram_tensor(f"kv_in{i}", [1024, 1024], BF16) for i in range(4)]
    kv_all_t = [nc.dram_tensor(f"kv_all{i}", [2048, 1024], BF16) for i in range(4)]
    lf_in_t = nc.dram_tensor("lf_in", [128, 96], F32)
    lf_all_t = nc.dram_tensor("lf_all", [256, 96], F32)
    cqT_t = nc.dram_tensor("cqT_s", [96, 128], F32)
    kv_in = [t.ap() for t in kv_in_t]
    kv_all = [t.ap() for t in kv_all_t]

    def kT_in(jj, ti):
        return kv_in[ti][jj * 128:(jj + 1) * 128, :]

    def kT_all(jj, r, ti):
        return kv_all[ti][r * 1024 + jj * 128: r * 1024 + (jj + 1) * 128, :]

    def v_in(ti):
        return kv_in[2 + ti]

    def v_all(ti, r):
        return kv_all[2 + ti][r * 1024:(r + 1) * 1024, :]

    RG = [[0, 1], [2, 3], [4, 5], [6, 7]]

    with ExitStack() as top:
        P = Prog(nc, top)
        P.stop = stop
        P.single_block = True
        uid = [0]

        def sb(st, name, shape, dt):
            uid[0] += 1
            return st.enter_context(nc.sbuf_tensor(f"sb{uid[0]}_{name}", list(shape), dt))

        def ps(st, name, shape, dt):
            uid[0] += 1
            return st.enter_context(nc.psum_tensor(f"ps{uid[0]}_{name}", list(shape), dt))

        def dma(eng, out, in_, reads, writes, dbuf):
            return P.op(eng, lambda e: e.dma_start(out=out, in_=in_), reads=reads, writes=writes,
                        dma=True, dbuf=dbuf)

        ident_f = sb(top, "ident_f", [128, 128], F32)
        ident_b = sb(top, "ident_b", [128, 128], BF16)
        ones_f = sb(top, "ones_f", [128, 128], F32)
        ones_b = sb(top, "ones_b", [128, 128], BF16)
        tri_f = sb(top, "tri_f", [128, 128], F32)
        hg = sb(top, "hg", [128, 6], F32)
        hgq = sb(top, "hgq", [128, 3], F32)
        fbias = sb(top, "fbias", [128, TILE_TB * 6], F32)
        esink = sb(top, "esink", [128, 6], F32)
        hsel = sb(top, "hsel", [128, 1], F32)
        lf_sb = sb(top, "lf_sb", [128, NTB * 6], F32)
        B_c = P.buf("consts")
        B_lf = P.buf("lf_sb")
        for t_sb, t_d in ((ident_f, ident_d), (ones_f, ones_d), (tri_f, tri_d),
                          (hg, hg_d), (fbias, fb_d), (esink, sinks_d), (hsel, hsel_d)):
            dma("sync", t_sb[:], t_d, [], [B_c], B_c)
        P.op("vector", lambda e: e.tensor_copy(out=ident_b[:], in_=ident_f[:]), reads=[B_c], writes=[B_c])
        P.op("vector", lambda e: e.tensor_copy(out=ones_b[:], in_=ones_f[:]), reads=[B_c], writes=[B_c])
        for i in range(3):
            P.op("vector", lambda e, i=i: e.scalar_tensor_tensor(
                out=hgq[:, i:i + 1], in0=hg[:, 2 * i:2 * i + 1], scalar=SCALE, in1=hg[:, 2 * i + 1:2 * i + 2],
                op0=ALU.mult, op1=ALU.mult), reads=[B_c], writes=[B_c])
        P.op("scalar", lambda e: e.activation(out=esink[:], in_=esink[:], func=AF.Exp), reads=[B_c], writes=[B_c])

        def norm_transpose(st, src_fn, ntb, gi, hT, tag, collect=None, npt=4):
            NX = 3
            xs = [sb(st, f"{tag}xs{i}", [128, D], F32) for i in range(NX)]
            xn = [sb(st, f"{tag}xn{i}", [128, D], BF16) for i in range(2)]
            grow = sb(st, f"{tag}grow", [128, D], F32)
            junk = sb(st, f"{tag}junk", [128, D], BF16)
            ssq = [sb(st, f"{tag}ssq{i}", [128, 1], F32) for i in range(2)]
            rstd = [sb(st, f"{tag}rstd{i}", [128, 1], F32) for i in range(2)]
            pT = [ps(st, f"{tag}pT{i}", [128, 4, 128], BF16) for i in range(npt)]
            Bxn, Bss, Brs = (P.bufs_n(tag + n, 2) for n in ("xn", "ss", "rs"))
            Bxs = P.bufs_n(tag + "xs", NX)
            BpT = P.bufs_n(tag + "pT", npt)
            Bjunk, Bgrow = P.buf(tag + "junk"), P.buf(tag + "grow")
            dma("sync", grow[:], grow_d[gi], [], [Bgrow], Bgrow)
            kc = [0]

            def stage1(tb):
                i = tb % 2
                x_ = xs[tb % NX]
                Bx = Bxs[tb % NX]
                dma("sync", x_[:], src_fn(tb), [], [Bx], Bx)
                P.op("scalar", lambda e: e.activation(out=junk[:], in_=x_[:], func=AF.Square, accum_out=ssq[i][:]),
                     reads=[Bx], writes=[Bjunk, Bss[i]])
                P.op("scalar", lambda e: e.activation(out=rstd[i][:], in_=ssq[i][:], func=AF.Ln, bias=EPS, scale=1.0 / D),
                     reads=[Bss[i]], writes=[Brs[i]])
                P.op("scalar", lambda e: e.activation(out=rstd[i][:], in_=rstd[i][:], func=AF.Exp, scale=-0.5),
                     reads=[Brs[i]], writes=[Brs[i]])
                P.op("vector", lambda e: e.scalar_tensor_tensor(
                    out=xn[i][:], in0=x_[:], scalar=rstd[i][:, 0:1], in1=grow[:], op0=ALU.mult, op1=ALU.mult),
                    reads=[Bx, Brs[i], Bgrow], writes=[Bxn[i]])

            def stage2(tb):
                i = tb % 2
                for cg in range(4):
                    kk = kc[0] % npt
                    kc[0] += 1
                    for cc in range(4):
                        c = cg * 4 + cc
                        P.op("tensor", lambda e, c=c, cc=cc, kk=kk: e.transpose(
                            out=pT[kk][:, cc, :], in_=xn[i][:, c * 128:(c + 1) * 128], identity=ident_b[:]),
                            reads=[Bxn[i], B_c], writes=[BpT[kk]])
                    dst = hT[:, cg * 4:(cg + 1) * 4, tb * 128:(tb + 1) * 128]
                    Bd = P.buf(tag + "hTp")
                    if collect is not None:
                        collect[tb].append(Bd)
                    if cg % 2 == 0:
                        P.op("scalar", lambda e, dst=dst, kk=kk: e.copy(out=dst, in_=pT[kk][:]),
                             reads=[BpT[kk]], writes=[Bd])
                    else:
                        P.op("vector", lambda e, dst=dst, kk=kk: e.tensor_copy(out=dst, in_=pT[kk][:]),
                             reads=[BpT[kk]], writes=[Bd])

            for tb in range(ntb):
                stage1(tb)
                if tb >= 1:
                    stage2(tb - 1)
            stage2(ntb - 1)

        def ffn_tile(src_rows, dst_rows, row0, gi, wgu, wd, tag, hooks=None):
            with ExitStack() as s_at:
                AT = sb(s_at, tag + "AT", [128, NFB, TT], BF16)
                with ExitStack() as s_h:
                    hT = sb(s_h, tag + "hT", [128, NCH, TT], BF16)
                    with ExitStack() as s2:
                        BhTtb = [[] for _ in range(TILE_TB)]
                        norm_transpose(s2, lambda tb: src_rows[row0 + tb * 128: row0 + (tb + 1) * 128, :],
                                       TILE_TB, gi, hT, tag + "n", collect=BhTtb)
                        BhT_half = [[b for tb in range(4 * hf, 4 * hf + 4) for b in BhTtb[tb]] for hf in range(2)]
                        NS = 3
                        wsl = [sb(s2, f"{tag}wgu{i}", [128, 2, NCH, 128], BF16) for i in range(NS)]
                        sg = [sb(s2, f"{tag}sg{i}", [128, 512], F32) for i in range(2)]
                        pg = [ps(s2, f"{tag}pg{i}", [128, 512], F32) for i in range(2)]
                        pu = [ps(s2, f"{tag}pu{i}", [128, 512], F32) for i in range(2)]
                        Bw = P.bufs_n(tag + "wgu", NS)
                        Bsg, Bpg, Bpu = (P.bufs_n(tag + n, 2) for n in ("sg", "pg", "pu"))
                        k = 0
                        for fb in range(NFB):
                            sl = fb % NS
                            if hooks and fb in hooks:
                                hooks[fb]()
                            dma("gpsimd", wsl[sl][:],
                                wgu[fb].rearrange("p (a c j) -> p a c j", a=2, c=NCH), [], [Bw[sl]], Bw[sl])
                            for half in range(2):
                                kk = k % 2
                                k += 1
                                rhs_sl = slice(half * 512, (half + 1) * 512)
                                for a, pp_, Bp in ((0, pg, Bpg), (1, pu, Bpu)):
                                    for c in range(NCH):
                                        P.op("tensor", lambda e, a=a, c=c, sl=sl, kk=kk, pp_=pp_, rhs_sl=rhs_sl: e.matmul(
                                            pp_[kk][:], lhsT=wsl[sl][:, a, c, :], rhs=hT[:, c, rhs_sl],
                                            start=(c == 0), stop=(c == NCH - 1)),
                                            reads=[Bw[sl]] + BhT_half[half], writes=[Bp[kk]])
                                P.op("scalar", lambda e, kk=kk: e.activation(out=sg[kk][:], in_=pg[kk][:], func=AF.Silu),
                                     reads=[Bpg[kk]], writes=[Bsg[kk]])
                                P.op("vector", lambda e, kk=kk, fb=fb, rhs_sl=rhs_sl: e.tensor_tensor(
                                    out=AT[:, fb, rhs_sl], in0=sg[kk][:], in1=pu[kk][:], op=ALU.mult),
                                    reads=[Bsg[kk], Bpu[kk]], writes=[P.buf(tag + "ATp")])
                        P.end_phase()
                with ExitStack() as s3:
                    NS = 3
                    wdl = [sb(s3, f"{tag}wd{i}", [128, NFB, 128], BF16) for i in range(NS)]
                    xsp = [sb(s3, f"{tag}xsp{i}", [128, TILE_TB, 512], F32) for i in range(2)]
                    yT = [sb(s3, f"{tag}yT{i}", [128, 512], F32) for i in range(3)]
                    py = [ps(s3, f"{tag}py{i}", [128, 512], F32) for i in range(3)]
                    pyt = [ps(s3, f"{tag}pyt{i}", [128, 4, 128], F32) for i in range(2)]
                    Bw = P.bufs_n(tag + "wd", NS)
                    Bxsp, Bpyt = (P.bufs_n(tag + n, 2) for n in ("xsp", "pyt"))
                    ByT, Bpy = (P.bufs_n(tag + n, 3) for n in ("yT", "py"))
                    BAT = P.buf(tag + "ATr")
                    pending = []

                    def finish(item):
                        k_, sd_, half_, dbl_, last_ = item
                        k3, k2 = k_ % 3, k_ % 2
                        for tt in range(4):
                            P.op("tensor", lambda e, k3=k3, k2=k2, tt=tt: e.transpose(
                                out=pyt[k2][:, tt, :], in_=yT[k3][:, tt * 128:(tt + 1) * 128], identity=ident_f[:]),
                                reads=[ByT[k3]], writes=[Bpyt[k2]])
                        dst = xsp[sd_][:, half_ * 4:(half_ + 1) * 4, dbl_ * 128:(dbl_ + 1) * 128]
                        P.op("vector", lambda e, k2=k2, dst=dst: e.scalar_tensor_tensor(
                            out=dst, in0=pyt[k2][:], scalar=0.5, in1=dst, op0=ALU.mult, op1=ALU.add),
                            reads=[Bpyt[k2], Bxsp[sd_]], writes=[Bxsp[sd_]])
                        if last_ is not None:
                            dma("sync", last_, xsp[sd_][:], [Bxsp[sd_]], [], Bxsp[sd_])

                    k = 0
                    for dg in range(4):
                        sd = dg % 2
                        csl = slice(dg * 512, (dg + 1) * 512)
                        dma("sync", xsp[sd][:],
                            src_rows[row0:row0 + TT, csl].rearrange("(tb p) c -> p tb c", p=128),
                            [], [Bxsp[sd]], Bxsp[sd])
                        for dbl in range(4):
                            db = dg * 4 + dbl
                            sl = db % NS
                            dma("gpsimd", wdl[sl][:], wd[db].rearrange("p (f j) -> p f j", f=NFB),
                                [], [Bw[sl]], Bw[sl])
                            for half in range(2):
                                k3 = k % 3
                                for fc in range(NFB):
                                    P.op("tensor", lambda e, fc=fc, sl=sl, k3=k3, half=half: e.matmul(
                                        py[k3][:], lhsT=wdl[sl][:, fc, :], rhs=AT[:, fc, half * 512:(half + 1) * 512],
                                        start=(fc == 0), stop=(fc == NFB - 1)),
                                        reads=[Bw[sl], BAT], writes=[Bpy[k3]])
                                P.op("scalar", lambda e, k3=k3: e.copy(out=yT[k3][:], in_=py[k3][:]),
                                     reads=[Bpy[k3]], writes=[ByT[k3]])
                                last = None
                                if dbl == 3 and half == 1:
                                    last = dst_rows[row0:row0 + TT, csl].rearrange("(tb p) c -> p tb c", p=128)
                                pending.append((k, sd, half, dbl, last))
                                k += 1
                                if len(pending) > 1:
                                    finish(pending.pop(0))
                    while pending:
                        finish(pending.pop(0))
                    P.end_phase()

        def qk_heads(st, hT, BhT_fn, ncols, heads, tag, npq=3):
            pq = [ps(st, f"{tag}pq{i}", [128, 512], F32) for i in range(npq)]
            pss = [ps(st, f"{tag}pss{i}", [128, 512], F32) for i in range(2)]
            qs = [sb(st, f"{tag}qs{i}", [128, 512], F32) for i in range(2)]
            sq = [sb(st, f"{tag}sq{i}", [128, 512], F32) for i in range(2)]
            rst = [sb(st, f"{tag}rst{i}", [128, 512], F32) for i in range(2)]
            qo = [sb(st, f"{tag}qo{i}", [128, 512], BF16) for i in range(3)]
            Bpq, Bqo = P.bufs_n(tag + "pq", npq), P.bufs_n(tag + "qo", 3)
            Bpss, Bsq, Brst, Bqs = (P.bufs_n(tag + n, 2) for n in ("pss", "sq", "rst", "qs"))
            pieces = [(c0, min(512, ncols - c0)) for c0 in range(0, ncols, 512)]
            pend = []

            def finish(item):
                k_, n_, gain_, dst_ = item
                k3, k2 = k_ % 3, k_ % 2
                P.op("tensor", lambda e: e.matmul(pss[k2][:, 0:n_], lhsT=ones_f[:], rhs=sq[k2][:, 0:n_], start=True, stop=True),
                     reads=[Bsq[k2], B_c], writes=[Bpss[k2]])
                P.op("scalar", lambda e: e.activation(out=rst[k2][:, 0:n_], in_=pss[k2][:, 0:n_], func=AF.Ln,
                                                      bias=EPS, scale=1.0 / HD),
                     reads=[Bpss[k2]], writes=[Brst[k2]])
                P.op("scalar", lambda e: e.activation(out=rst[k2][:, 0:n_], in_=rst[k2][:, 0:n_], func=AF.Exp, scale=-0.5),
                     reads=[Brst[k2]], writes=[Brst[k2]])
                dst, is_sbuf, Bd = dst_
                out_ap = dst if is_sbuf else qo[k3][:, 0:n_]
                wr = [Bd] if is_sbuf else [Bqo[k3]]
                if gain_ is not None:
                    P.op("vector", lambda e: e.scalar_tensor_tensor(
                        out=out_ap, in0=qs[k2][:, 0:n_], scalar=gain_, in1=rst[k2][:, 0:n_], op0=ALU.mult, op1=ALU.mult),
                        reads=[Bqs[k2], Brst[k2], B_c], writes=wr)
                else:
                    P.op("vector", lambda e: e.tensor_tensor(out=out_ap, in0=qs[k2][:, 0:n_], in1=rst[k2][:, 0:n_], op=ALU.mult),
                         reads=[Bqs[k2], Brst[k2]], writes=wr)
                if not is_sbuf:
                    dma("sync", dst, qo[k3][:, 0:n_], [Bqo[k3]], [], Bqo[k3])

            k = 0
            for (pre_fn, w_ap_fn, Bw, gain, dst_fn) in heads:
                if pre_fn is not None:
                    pre_fn()
                for (c0, n) in pieces:
                    kq_, k2 = k % npq, k % 2
                    for c in range(NCH):
                        P.op("tensor", lambda e, c=c, kq_=kq_, c0=c0, n=n, w_ap_fn=w_ap_fn: e.matmul(
                            pq[kq_][:, 0:n], lhsT=w_ap_fn(c), rhs=hT[:, c, c0:c0 + n], start=(c == 0), stop=(c == NCH - 1)),
                            reads=[Bw] + BhT_fn(c0, n), writes=[Bpq[kq_]])
                    P.op("scalar", lambda e, kq_=kq_, k2=k2, n=n: e.copy(out=qs[k2][:, 0:n], in_=pq[kq_][:, 0:n]),
                         reads=[Bpq[kq_]], writes=[Bqs[k2]])
                    P.op("vector", lambda e, k2=k2, n=n: e.tensor_tensor(out=sq[k2][:, 0:n], in0=qs[k2][:, 0:n],
                                                                         in1=qs[k2][:, 0:n], op=ALU.mult),
                         reads=[Bqs[k2]], writes=[Bsq[k2]])
                    pend.append((k, n, gain, dst_fn(c0, n)))
                    k += 1
                    if len(pend) > 1:
                        finish(pend.pop(0))
            while pend:
                finish(pend.pop(0))
            return pq, Bpq

        B_g1 = P.buf("g1")

        def exchange(chunks):
            for i in chunks:
                P.op("gpsimd", lambda e, i=i: e.collective_compute("AllGather", ALU.bypass, replica_groups=RG,
                                                                   ins=[kv_in_t[i].ap().opt()], outs=[kv_all_t[i].ap().opt()]),
                     writes=[B_g1], dma=True, dbuf=B_g1, ninc=1)

        for ti in range(NTILE):
            row0 = ti * TT
            hooks = {3: (lambda: exchange((0,))), 9: (lambda: exchange((2,)))} if ti == 1 else None
            ffn_tile(x_d, x1_d, row0, 0, wgu_d[0], wd_d[0], f"f1t{ti}", hooks=hooks)
            def proj_tile(ti, row0):
                with ExitStack() as s_h:
                    h2T = sb(s_h, f"p{ti}h2T", [128, NCH, TT], BF16)
                    with ExitStack() as s2:
                        tag = f"p{ti}"
                        win = [sb(s2, f"{tag}win{i}", [128, NCH, 512], BF16) for i in range(2)]
                        Bwin = P.bufs_n(tag + "win", 2)
                        Bh2tb = [[] for _ in range(TILE_TB)]
                        norm_transpose(s2, lambda tb: x1_d[row0 + tb * 128: row0 + (tb + 1) * 128, :],
                                       TILE_TB, 1, h2T, f"p{ti}n", collect=Bh2tb, npt=2)
                        Bh2_fn = lambda c0, n: [b for tb in range(c0 // 128, (c0 + n) // 128) for b in Bh2tb[tb]]

                        def load_group(g):
                            sl = g % 2
                            dma("gpsimd", win[sl][:], win_d[g].rearrange("p (c j) -> p c j", c=NCH), [], [Bwin[sl]], Bwin[sl])

                        heads = []
                        load_group(0)
                        for hidx in range(24):
                            g, hh = hidx // 4, hidx % 4
                            sl = g % 2
                            if hidx < 6:
                                gain, drow = hgq[:, 0:1], qT_d[hidx * 128:(hidx + 1) * 128, row0:row0 + TT]
                            elif hidx < 12:
                                gain, drow = None, kT_in(hidx - 6, ti)
                            elif hidx < 18:
                                gain, drow = hgq[:, 1:2], qT_d[(hidx - 6) * 128:(hidx - 5) * 128, row0:row0 + TT]
                            elif hidx < 20:
                                gain, drow = None, kT_in(hidx - 12, ti)
                            else:
                                gain, drow = hgq[:, 2:3], qT_d[(hidx - 8) * 128:(hidx - 7) * 128, row0:row0 + TT]
                            heads.append((
                                (lambda g=g: load_group(g + 1)) if hh == 0 else None,
                                (lambda c, sl=sl, hh=hh: win[sl][:, c, hh * 128:(hh + 1) * 128]),
                                Bwin[sl], gain,
                                (lambda c0, n, drow=drow: (drow[:, c0:c0 + n], False, None))))
                        pq_, Bpq_ = qk_heads(s2, h2T, Bh2_fn, TT, heads, tag + "q")
                        vst = [sb(s2, f"{tag}vst{i}", [128, TILE_TB, 512], BF16) for i in range(2)]
                        wfl = sb(s2, tag + "wfl", [128, NCH, 6], BF16)
                        zt = sb(s2, tag + "zt", [128, TILE_TB, 6], F32)
                        pp, Bpp = pq_, Bpq_
                        pz = ps(s2, tag + "pz", [128, TILE_TB, 8], F32)
                        Bvst = P.bufs_n(tag + "vst", 2)
                        Bwfl, Bpz, Bzt = P.buf(tag + "wfl"), P.buf(tag + "pz"), P.buf(tag + "zt")
                        dma("gpsimd", wfl[:], wfl_d.rearrange("p (c j) -> p c j", c=NCH), [], [Bwfl], Bwfl)
                        k = 0
                        for g in (6, 7):
                            sl = g % 2
                            if g == 6:
                                load_group(7)
                            for tb in range(TILE_TB):
                                kk = k % 2
                                k += 1
                                tsl = slice(tb * 128, (tb + 1) * 128)
                                for c in range(NCH):
                                    P.op("tensor", lambda e, c=c, sl=sl, kk=kk, tsl=tsl: e.matmul(
                                        pp[kk][:], lhsT=h2T[:, c, tsl], rhs=win[sl][:, c, :], start=(c == 0), stop=(c == NCH - 1)),
                                        reads=[Bwin[sl]] + Bh2tb[tb], writes=[Bpp[kk]])
                                if g == 6:
                                    for c in range(NCH):
                                        P.op("tensor", lambda e, c=c, tb=tb, tsl=tsl: e.matmul(
                                            pz[:, tb, 0:6], lhsT=h2T[:, c, tsl], rhs=wfl[:, c, :], start=(c == 0), stop=(c == NCH - 1)),
                                            reads=[Bwfl] + Bh2tb[tb], writes=[Bpz])
                                if tb % 2 == 0:
                                    P.op("scalar", lambda e, kk=kk, sl=sl, tb=tb: e.copy(out=vst[sl][:, tb, :], in_=pp[kk][:]),
                                         reads=[Bpp[kk]], writes=[Bvst[sl]])
                                else:
                                    P.op("vector", lambda e, kk=kk, sl=sl, tb=tb: e.tensor_copy(out=vst[sl][:, tb, :], in_=pp[kk][:]),
                                         reads=[Bpp[kk]], writes=[Bvst[sl]])
                            dma("sync", v_in(ti)[:, (g - 6) * 512:(g - 5) * 512].rearrange("(tb p) c -> p tb c", p=128),
                                vst[sl][:], [Bvst[sl]], [], Bvst[sl])
                        P.op("vector", lambda e: e.tensor_tensor(out=zt[:], in0=pz[:, :, 0:6],
                                                                 in1=fbias[:].rearrange("p (t j) -> p t j", j=6), op=ALU.add),
                             reads=[Bpz], writes=[Bzt])
                        P.op("scalar", lambda e: e.activation(out=zt[:], in_=zt[:], func=AF.Exp, scale=-1.0), reads=[Bzt], writes=[Bzt])
                        P.op("scalar", lambda e: e.activation(out=zt[:], in_=zt[:], func=AF.Ln, bias=1.0), reads=[Bzt], writes=[Bzt])
                        P.op("vector", lambda e, ti=ti: e.tensor_scalar(
                            out=lf_sb[:, ti * TILE_TB * 6:(ti + 1) * TILE_TB * 6].rearrange("p (t j) -> p t j", j=6),
                            in0=zt[:], scalar1=-1.0, scalar2=None, op0=ALU.mult),
                            reads=[Bzt], writes=[B_lf])
                        P.end_phase()

            proj_tile(ti, row0)
        mkT = sb(top, "mkT", [128, 4, 256], BF16)
        mv = sb(top, "mv", [128, 2, 512], BF16)
        def phase_b():
            with ExitStack() as sB:
                B_lfin = P.buf("lfin")
                B_g2 = P.buf("g2")
                mnT = sb(sB, "mnT", [128, NCH, 256], BF16)
                wm = [sb(sB, f"wm{i}", [128, NCH, 512], BF16) for i in range(2)]
                pp = [ps(sB, f"mpp{i}", [128, 512], F32) for i in range(2)]
                Bwm, Bpp = P.bufs_n("wm", 2), P.bufs_n("mpp", 2)
                BmkT, Bmv = P.buf("mkT"), P.buf("mv")
                dma("sync", lf_in_t.ap(), lf_sb[:], [B_lf], [B_lfin], B_lfin)
                for i in range(2):
                    dma("gpsimd", wm[i][:], wmem_d[i].rearrange("p (c j) -> p c j", c=NCH), [], [Bwm[i]], Bwm[i])
                exchange((1, 3))
                P.op("gpsimd", lambda e: e.collective_compute("AllGather", ALU.bypass, replica_groups=RG,
                                                              ins=[lf_in_t.ap().opt()], outs=[lf_all_t.ap().opt()]),
                     reads=[B_lfin], writes=[B_g2], dma=True, dbuf=B_g2, ninc=1)
                BmnTtb = [[], []]
                norm_transpose(sB, lambda tb: mem_d[tb * 128:(tb + 1) * 128, :], 2, 2, mnT, "mn", collect=BmnTtb, npt=2)
                BmnT = BmnTtb[0] + BmnTtb[1]
                heads = []
                for hh in range(4):
                    heads.append((None, (lambda c, hh=hh: wm[0][:, c, hh * 128:(hh + 1) * 128]), Bwm[0], None,
                                  (lambda c0, n, hh=hh: (mkT[:, hh, c0:c0 + n], True, BmkT))))
                qk_heads(sB, mnT, (lambda c0, n: BmnT), 256, heads, "mk", npq=2)
                for tb in range(2):
                    for c in range(NCH):
                        P.op("tensor", lambda e, c=c, tb=tb: e.matmul(
                            pp[tb][:], lhsT=mnT[:, c, tb * 128:(tb + 1) * 128], rhs=wm[1][:, c, :],
                            start=(c == 0), stop=(c == NCH - 1)),
                            reads=[Bwm[1]] + BmnT, writes=[Bpp[tb]])
                    P.op("vector", lambda e, tb=tb: e.tensor_copy(out=mv[:, tb, :], in_=pp[tb][:]),
                         reads=[Bpp[tb]], writes=[Bmv])
                P.end_phase()

        phase_b()

        with ExitStack() as s_c:
            mixT = sb(s_c, "mixT", [128, 16, NTOK], BF16)
            negck = sb(s_c, "negck", [128, 32 * 6], F32)
            cq = sb(s_c, "cq", [128, NTB * 6], F32)
            fmask = sb(s_c, "fmask", [128, 2, 128], F32)
            fmask_b = sb(s_c, "fmask_b", [128, 2, 128], BF16)
            Bcum = P.buf("cum")
            with ExitStack() as s_vf:
                vf = sb(s_vf, "vf", [128, 2, 16, 768], BF16)
                Bvf = P.buf("vf")
                def phase_c0():
                    with ExitStack() as s0:
                        lf = sb(s0, "lf", [128, 32, 6], F32)
                        tot = sb(s0, "tot", [128, 32, 6], F32)
                        pre = sb(s0, "pre", [128, 32, 6], F32)
                        ck = sb(s0, "ck", [128, 32, 6], F32)
                        dlt = sb(s0, "dlt", [128, 16, 6], F32)
                        pc = ps(s0, "pc", [128, 192], F32)
                        ptot = ps(s0, "ptot", [128, 192], F32)
                        Blfl, Btot, Bpre, Bpc, Bptot = (P.buf(n) for n in ("lfl", "tot", "pre", "pc", "ptot"))
                        dma("sync", fmask[:], fmask_d.rearrange("p (a t) -> p a t", a=2), [], [Bcum], Bcum)
                        P.op("vector", lambda e: e.tensor_copy(out=fmask_b[:], in_=fmask[:]), reads=[Bcum], writes=[Bcum])
                        lfv = lf[:].rearrange("p (pl r) j -> p pl r j", r=2)
                        for r in range(2):
                            dma("sync", lfv[:, :, r, :], lf_all_t.ap()[r * 128:(r + 1) * 128, :].rearrange("p (pl j) -> p pl j", j=6),
                                [], [Blfl], Blfl)
                        for r in range(2):
                            for tih in range(2):
                                dma("sync", vf[:, r, tih * 8:(tih + 1) * 8, :],
                                    v_all(tih, r)[:, 0:768].rearrange("(pl p) c -> p pl c", p=128), [], [Bvf], Bvf)
                        lf2 = lf[:].rearrange("p g j -> p (g j)")
                        P.op("tensor", lambda e: e.matmul(pc[:], lhsT=tri_f[:], rhs=lf2, start=True, stop=True),
                             reads=[Blfl], writes=[Bpc])
                        P.op("tensor", lambda e: e.matmul(ptot[:], lhsT=ones_f[:], rhs=lf2, start=True, stop=True),
                             reads=[Blfl], writes=[Bptot])
                        P.op("vector", lambda e: e.tensor_copy(out=tot[:].rearrange("p g j -> p (g j)"), in_=ptot[:]),
                             reads=[Bptot], writes=[Btot])
                        scan = [tot, sb(s0, "scan1", [128, 32, 6], F32)]
                        cur = 0
                        for stp in (1, 2, 4, 8, 16):
                            src_, dst_ = scan[cur], scan[1 - cur]
                            P.op("vector", lambda e, src_=src_, dst_=dst_, stp=stp: e.tensor_tensor(
                                out=dst_[:, stp:32, :], in0=src_[:, stp:32, :], in1=src_[:, 0:32 - stp, :], op=ALU.add),
                                reads=[Btot], writes=[Btot])
                            P.op("vector", lambda e, src_=src_, dst_=dst_, stp=stp: e.tensor_copy(
                                out=dst_[:, 0:stp, :], in_=src_[:, 0:stp, :]), reads=[Btot], writes=[Btot])
                            cur = 1 - cur
                        incl = scan[cur]
                        P.op("vector", lambda e: e.memset(pre[:, 0, :], 0.0), writes=[Bpre])
                        P.op("vector", lambda e: e.tensor_copy(out=pre[:, 1:32, :], in_=incl[:, 0:31, :]), reads=[Btot], writes=[Bpre])
                        P.op("vector", lambda e: e.tensor_tensor(out=ck[:].rearrange("p g j -> p (g j)"), in0=pc[:],
                                                                 in1=pre[:].rearrange("p g j -> p (g j)"), op=ALU.add),
                             reads=[Bpc, Bpre], writes=[Bcum])
                        P.op("vector", lambda e: e.tensor_scalar(out=negck[:], in0=ck[:].rearrange("p g j -> p (g j)"),
                                                                 scalar1=-1.0, scalar2=None, op0=ALU.mult),
                             reads=[Bcum], writes=[Bcum])
                        ckv = ck[:].rearrange("p (pl r) j -> p pl r j", r=2)
                        P.op("vector", lambda e: e.tensor_tensor(out=dlt[:], in0=ckv[:, :, 1, :], in1=ckv[:, :, 0, :], op=ALU.subtract),
                             reads=[Bcum], writes=[Bcum])
                        P.op("vector", lambda e: e.scalar_tensor_tensor(
                            out=cq[:].rearrange("p (pl j) -> p pl j", j=6), in0=dlt[:], scalar=hsel[:, 0:1],
                            in1=ckv[:, :, 0, :], op0=ALU.mult, op1=ALU.add),
                            reads=[Bcum], writes=[Bcum])
                        pcq = ps(s0, "pcq", [96, 128], F32)
                        cqT = sb(s0, "cqT", [96, 128], F32)
                        Bpcq, BcqT = P.buf("pcq"), P.buf("cqT")
                        P.op("tensor", lambda e: e.transpose(out=pcq[:], in_=cq[:], identity=ident_f[:]), reads=[Bcum], writes=[Bpcq])
                        P.op("vector", lambda e: e.tensor_copy(out=cqT[:], in_=pcq[:]), reads=[Bpcq], writes=[BcqT])
                        dma("sync", cqT_t.ap(), cqT[:], [BcqT], [], BcqT)
                        P.end_phase()

                phase_c0()

                def attn_epilogue(po_ap, pl_ap, dst_ap, rl_ap, Bpo, Bpl, Brl, extra=None):
                    if extra is not None:
                        P.op("scalar", lambda e: e.activation(out=rl_ap, in_=pl_ap, func=AF.Ln, bias=extra),
                             reads=[Bpl], writes=[Brl])
                    else:
                        P.op("scalar", lambda e: e.activation(out=rl_ap, in_=pl_ap, func=AF.Ln), reads=[Bpl], writes=[Brl])
                    P.op("scalar", lambda e: e.activation(out=rl_ap, in_=rl_ap, func=AF.Exp, scale=-1.0), reads=[Brl], writes=[Brl])
                    P.op("vector", lambda e: e.tensor_tensor(out=dst_ap, in0=po_ap, in1=rl_ap, op=ALU.mult),
                         reads=[Bpo, Brl], writes=[P.buf("mixTp")])

                def phase_c1():
                    with ExitStack() as s1:
                        kT = [sb(s1, f"kT{i}", [128, 2, NTOK], BF16) for i in range(2)]
                        qT = [sb(s1, f"qT{i}", [128, NTOK], BF16) for i in range(2)]
                        CT = [sb(s1, f"CT{i}", [128, NTOK], F32) for i in range(2)]
                        NB = 4
                        X = [sb(s1, f"X{i}", [128, 512], F32) for i in range(NB)]
                        PT = [sb(s1, f"PT{i}", [128, 512], BF16) for i in range(NB)]
                        rl = [sb(s1, f"rl{i}", [128, 512], F32) for i in range(2)]
                        pS = [ps(s1, f"pS{i}", [128, 512], F32) for i in range(NB)]
                        pO = [ps(s1, f"pO{i}", [128, 512], F32) for i in range(2)]
                        pL = [ps(s1, f"pL{i}", [128, 512], F32) for i in range(2)]
                        BkT, BqT, BCT, Brl, BpO, BpL = (P.bufs_n(n, 2) for n in ("kT", "qT", "CT", "rl", "pO", "pL"))
                        BX, BPT, BpS = (P.bufs_n(n, NB) for n in ("X", "PT", "pS"))
                        def load_head(j):
                            hb = j % 2
                            for r in range(2):
                                for tih in range(2):
                                    dma("sync", kT[hb][:, r, tih * TT:(tih + 1) * TT], kT_all(j, r, tih), [], [BkT[hb]], BkT[hb])
                            dma("sync", qT[hb][:], qT_d[j * 128:(j + 1) * 128, :], [], [BqT[hb]], BqT[hb])
                            dma("sync", CT[hb][:].rearrange("p (a b) -> p a b", a=NTB),
                                bass.AP(cqT_t, j * 128, [[0, 128], [6 * 128, NTB], [1, 128]]), [], [BCT[hb]], BCT[hb])

                        load_head(0)
                        kq = 0
                        pend = []
                        for j in range(6):
                            hb = j % 2
                            if j + 1 < 6:
                                load_head(j + 1)
                            tiles = []
                            for qc in range(4):
                                ng = 8 * qc + 8
                                for g in range(ng):
                                    tiles.append((qc, g, ng))

                            def stage_a(qc, g, ng, kk, hb=hb, j=j):
                                r, pl_ = g % 2, g // 2
                                lo = max(pl_ - 4 * qc, 0) * 128
                                qsl = slice(qc * 512 + lo, (qc + 1) * 512)
                                diag = pl_ >= 4 * qc
                                P.op("tensor", lambda e: e.matmul(
                                    pS[kk][:, lo:512], lhsT=kT[hb][:, r, pl_ * 128:(pl_ + 1) * 128], rhs=qT[hb][:, qsl],
                                    start=True, stop=not diag),
                                    reads=[BkT[hb], BqT[hb]], writes=[BpS[kk]])
                                if diag:
                                    P.op("tensor", lambda e: e.matmul(
                                        pS[kk][:, lo:lo + 128], lhsT=ident_b[:], rhs=fmask_b[:, r, :], start=False, stop=True),
                                        reads=[Bcum], writes=[BpS[kk]])
                                P.op("vector", lambda e: e.tensor_tensor(
                                    out=X[kk][:, lo:512], in0=pS[kk][:, lo:512], in1=CT[hb][:, qsl], op=ALU.add),
                                    reads=[BpS[kk], BCT[hb]], writes=[BX[kk]])
                                P.op("scalar", lambda e: e.activation(
                                    out=PT[kk][:, lo:512], in_=X[kk][:, lo:512], func=AF.Exp,
                                    bias=negck[:, g * 6 + j:g * 6 + j + 1]),
                                    reads=[BX[kk], Bcum], writes=[BPT[kk]])

                            def stage_b(qc, g, ng, kk, j=j):
                                r, pl_ = g % 2, g // 2
                                lo = max(pl_ - 4 * qc, 0) * 128
                                ob = (j * 4 + qc) % 2
                                P.op("tensor", lambda e: e.matmul(
                                    pO[ob][:, lo:512], lhsT=vf[:, r, pl_, j * 128:(j + 1) * 128], rhs=PT[kk][:, lo:512],
                                    start=(g == 0), stop=(g == ng - 1)),
                                    reads=[Bvf, BPT[kk]], writes=[BpO[ob]])
                                P.op("tensor", lambda e: e.matmul(
                                    pL[ob][:, lo:512], lhsT=ones_b[:], rhs=PT[kk][:, lo:512],
                                    start=(g == 0), stop=(g == ng - 1)),
                                    reads=[BPT[kk]], writes=[BpL[ob]])
                                if g == ng - 1:
                                    attn_epilogue(pO[ob][:], pL[ob][:], mixT[:, j, qc * 512:(qc + 1) * 512], rl[ob][:],
                                                  BpO[ob], BpL[ob], Brl[ob])

                            for ti_, (qc, g, ng) in enumerate(tiles):
                                kk = kq % NB
                                kq += 1
                                stage_a(qc, g, ng, kk)
                                pend.append((stage_b, (qc, g, ng, kk)))
                                if len(pend) > NB - 1:
                                    fn_, a_ = pend.pop(0)
                                    fn_(*a_)
                        while pend:
                            fn_, a_ = pend.pop(0)
                            fn_(*a_)
                        P.end_phase()

                phase_c1()

            def phase_c2():
                with ExitStack() as s2:
                    vs = sb(s2, "vs", [128, 2, 16, 256], BF16)
                    skT = sb(s2, "skT", [128, 2, 2, NTOK], BF16)
                    qT = [sb(s2, f"sqT{i}", [128, NTOK], BF16) for i in range(2)]
                    sbias = sb(s2, "sbias", [128, 18, 512], F32)
                    sbias1 = sb(s2, "sbias1", [128, 18, 128], F32)
                    NB = 4
                    X = [sb(s2, f"sX{i}", [128, 512], F32) for i in range(NB)]
                    PT = [sb(s2, f"sPT{i}", [128, 512], BF16) for i in range(NB)]
                    rl = [sb(s2, f"srl{i}", [128, 512], F32) for i in range(2)]
                    pS = [ps(s2, f"spS{i}", [128, 512], F32) for i in range(NB)]
                    pO = [ps(s2, f"spO{i}", [128, 512], F32) for i in range(2)]
                    pL = [ps(s2, f"spL{i}", [128, 512], F32) for i in range(2)]
                    Bvs, BskT, Bsb = P.buf("vs"), P.buf("skT"), P.buf("sbias")
                    BqT, Brl, BpO, BpL = (P.bufs_n("s" + n, 2) for n in ("qT", "rl", "pO", "pL"))
                    BX, BPT, BpS = (P.bufs_n("s" + n, NB) for n in ("X", "PT", "pS"))
                    Bsb1 = P.buf("sbias1")
                    dma("sync", qT[0][:], qT_d[6 * 128:7 * 128, :], [], [BqT[0]], BqT[0])
                    for r in range(2):
                        for kvh in range(2):
                            for tih in range(2):
                                dma("sync", skT[:, kvh, r, tih * TT:(tih + 1) * TT], kT_all(6 + kvh, r, tih), [], [BskT], BskT)
                    dma("sync", sbias1[:], sbias_d.rearrange("p (a t) -> p a t", a=18), [], [Bsb1], Bsb1)
                    for pi in range(4):
                        P.op("gpsimd", lambda e, pi=pi: e.tensor_copy(out=sbias[:, :, pi * 128:(pi + 1) * 128], in_=sbias1[:]),
                             reads=[Bsb1], writes=[Bsb])
                    for r in range(2):
                        for tih in range(2):
                            dma("sync", vs[:, r, tih * 8:(tih + 1) * 8, :],
                                v_all(tih, r)[:, 768:1024].rearrange("(pl p) c -> p pl c", p=128), [], [Bvs], Bvs)
                    kq = 0
                    pend = []

                    def s_stage_a(j, qc, kind, kk):
                        hb, kvh = j % 2, j // 3
                        lo = 128 if (qc == 0 and kind == 0) else 0
                        for pi in range(lo // 128, 4):
                            p = qc * 4 + pi
                            g = 2 * p - 1 + kind
                            r, pl_ = g % 2, g // 2
                            P.op("tensor", lambda e, pi=pi, p=p, r=r, pl_=pl_: e.matmul(
                                pS[kk][:, pi * 128:(pi + 1) * 128], lhsT=skT[:, kvh, r, pl_ * 128:(pl_ + 1) * 128],
                                rhs=qT[hb][:, p * 128:(p + 1) * 128], start=True, stop=True),
                                reads=[BskT, BqT[hb]], writes=[BpS[kk]])
                        P.op("vector", lambda e: e.tensor_tensor(
                            out=X[kk][:, lo:512], in0=pS[kk][:, lo:512], in1=sbias[:, kind * 6 + j, lo:512], op=ALU.add),
                            reads=[BpS[kk], Bsb], writes=[BX[kk]])
                        P.op("scalar", lambda e: e.activation(out=PT[kk][:, lo:512], in_=X[kk][:, lo:512], func=AF.Exp),
                             reads=[BX[kk]], writes=[BPT[kk]])

                    def s_stage_b(j, qc, kind, kk):
                        kvh = j // 3
                        ob = (j * 4 + qc) % 2
                        lo = 128 if (qc == 0 and kind == 0) else 0
                        for pi in range(lo // 128, 4):
                            p = qc * 4 + pi
                            g = 2 * p - 1 + kind
                            r, pl_ = g % 2, g // 2
                            P.op("tensor", lambda e, pi=pi, r=r, pl_=pl_: e.matmul(
                                pO[ob][:, pi * 128:(pi + 1) * 128], lhsT=vs[:, r, pl_, kvh * 128:(kvh + 1) * 128],
                                rhs=PT[kk][:, pi * 128:(pi + 1) * 128], start=(kind == 1 and pi == 0), stop=(kind == 0 or (kind == 2 and p == 0)),
                                skip_group_check=True),
                                reads=[Bvs, BPT[kk]], writes=[BpO[ob]])
                        P.op("tensor", lambda e: e.matmul(
                            pL[ob][:, lo:512], lhsT=ones_b[:], rhs=PT[kk][:, lo:512], start=(kind == 1), stop=(kind == 0)),
                            reads=[BPT[kk]], writes=[BpL[ob]])
                        if kind == 0:
                            attn_epilogue(pO[ob][:], pL[ob][:], mixT[:, 6 + j, qc * 512:(qc + 1) * 512], rl[ob][:],
                                          BpO[ob], BpL[ob], Brl[ob], extra=esink[:, j:j + 1])

                    for j in range(6):
                        if j + 1 < 6:
                            hb1 = (j + 1) % 2
                            dma("sync", qT[hb1][:], qT_d[(7 + j) * 128:(8 + j) * 128, :], [], [BqT[hb1]], BqT[hb1])
                        for qc in range(4):
                            for kind in (1, 2, 0):
                                kk = kq % NB
                                kq += 1
                                s_stage_a(j, qc, kind, kk)
                                pend.append((j, qc, kind, kk))
                                if len(pend) > NB - 1:
                                    s_stage_b(*pend.pop(0))
                    while pend:
                        s_stage_b(*pend.pop(0))
                    P.end_phase()

            phase_c2()

            with ExitStack() as s_wo:
                wo = [sb(s_wo, f"wo{i}", [128, NCH, 512], BF16) for i in range(2)]
                xq = [sb(s_wo, f"oxq{i}", [128, NTB, 512], F32) for i in range(2)]
                Bwo, Bxq = P.bufs_n("wo", 2), P.bufs_n("oxq", 2)

                def load_q(q, xeng="sync"):
                    sl = q % 2
                    csl = slice(q * 512, (q + 1) * 512)
                    dma("gpsimd", wo[sl][:], wout_d[q].rearrange("p (c j) -> p c j", c=NCH), [], [Bwo[sl]], Bwo[sl])
                    dma(xeng, xq[sl][:], x1_d[:, csl].rearrange("(p t) c -> t p c", t=128), [], [Bxq[sl]], Bxq[sl])

                def phase_c3():
                    with ExitStack() as s3:
                        BmkT, Bmv = P.buf("mkT2"), P.buf("mv2")
                        with ExitStack() as s3d:
                            NB = 3
                            qT = [sb(s3d, f"mqT{i}", [128, NTOK], BF16) for i in range(2)]
                            PT = [sb(s3d, f"mPT{i}", [128, 512], BF16) for i in range(NB)]
                            rl = [sb(s3d, f"mrl{i}", [128, 512], F32) for i in range(2)]
                            pS = [ps(s3d, f"mpS{i}", [128, 512], F32) for i in range(NB)]
                            pO = [ps(s3d, f"mpO{i}", [128, 512], F32) for i in range(2)]
                            pL = [ps(s3d, f"mpL{i}", [128, 512], F32) for i in range(2)]
                            BqT, Brl, BpO, BpL = (P.bufs_n("m" + n, 2) for n in ("qT", "rl", "pO", "pL"))
                            BPT, BpS = (P.bufs_n("m" + n, NB) for n in ("PT", "pS"))
                            kq = 0
                            pend = []

                            def m_stage_b(j, qc, mb, kk):
                                ob = (j * 4 + qc) % 2
                                P.op("tensor", lambda e: e.matmul(
                                    pO[ob][:], lhsT=mv[:, mb, j * 128:(j + 1) * 128], rhs=PT[kk][:],
                                    start=(mb == 0), stop=(mb == 1)),
                                    reads=[Bmv, BPT[kk]], writes=[BpO[ob]])
                                P.op("tensor", lambda e: e.matmul(
                                    pL[ob][:], lhsT=ones_b[:], rhs=PT[kk][:], start=(mb == 0), stop=(mb == 1)),
                                    reads=[BPT[kk]], writes=[BpL[ob]])
                                if mb == 1:
                                    attn_epilogue(pO[ob][:], pL[ob][:], mixT[:, 12 + j, qc * 512:(qc + 1) * 512], rl[ob][:],
                                                  BpO[ob], BpL[ob], Brl[ob])

                            dma("sync", qT[0][:], qT_d[12 * 128:13 * 128, :], [], [BqT[0]], BqT[0])
                            for j in range(4):
                                hb = j % 2
                                if j + 1 < 4:
                                    hb1 = (j + 1) % 2
                                    dma("sync", qT[hb1][:], qT_d[(13 + j) * 128:(14 + j) * 128, :], [], [BqT[hb1]], BqT[hb1])
                                for qc in range(4):
                                    for mb in range(2):
                                        kk = kq % NB
                                        kq += 1
                                        P.op("tensor", lambda e, kk=kk, hb=hb, j=j, mb=mb, qc=qc: e.matmul(
                                            pS[kk][:], lhsT=mkT[:, j, mb * 128:(mb + 1) * 128], rhs=qT[hb][:, qc * 512:(qc + 1) * 512],
                                            start=True, stop=True),
                                            reads=[BmkT, BqT[hb]], writes=[BpS[kk]])
                                        P.op("scalar", lambda e, kk=kk: e.activation(out=PT[kk][:], in_=pS[kk][:], func=AF.Exp),
                                             reads=[BpS[kk]], writes=[BPT[kk]])
                                        pend.append((j, qc, mb, kk))
                                        if len(pend) > NB - 1:
                                            m_stage_b(*pend.pop(0))
                            while pend:
                                m_stage_b(*pend.pop(0))
                            P.end_phase()

                load_q(0, xeng="gpsimd")
                phase_c3()

                def phase_c4():
                    with ExitStack() as s4:
                        pp = [ps(s4, f"opp{i}", [128, 512], F32) for i in range(3)]
                        Bpp = P.bufs_n("opp", 3)
                        BmixT = P.buf("mixTr")
                        k = 0
                        for q in range(4):
                            sl = q % 2
                            csl = slice(q * 512, (q + 1) * 512)
                            if q + 1 < 4:
                                load_q(q + 1)
                            for p in range(NTB):
                                kk = k % 3
                                k += 1
                                rsl = slice(p * 128, (p + 1) * 128)
                                for c in range(16):
                                    P.op("tensor", lambda e, c=c, sl=sl, kk=kk, rsl=rsl: e.matmul(
                                        pp[kk][:], lhsT=mixT[:, c, rsl], rhs=wo[sl][:, c, :], start=(c == 0), stop=(c == 15)),
                                        reads=[Bwo[sl], BmixT], writes=[Bpp[kk]])
                                P.op("vector", lambda e, kk=kk, sl=sl, p=p: e.tensor_tensor(
                                    out=xq[sl][:, p, :], in0=pp[kk][:], in1=xq[sl][:, p, :], op=ALU.add),
                                    reads=[Bpp[kk], Bxq[sl]], writes=[Bxq[sl]])
                            dma("sync", x2_d[:, csl].rearrange("(p t) c -> t p c", t=128), xq[sl][:], [Bxq[sl]], [], Bxq[sl])
                        P.end_phase()

                phase_c4()

        for ti in range(NTILE):
            ffn_tile(x2_d, out_d, ti * TT, 3, wgu_d[1], wd_d[1], f"f2t{ti}")
        P.stopped = False
        for en in ENGINES:
            P.op(en, lambda e: e.nop())
        P.replay()
        K.nops = P.nops
    return nc


_NC_CACHE = {}


def _consts(h):
    ident = np.eye(128, dtype=np.float32)
    tri = np.triu(np.ones((128, 128), np.float32))
    ones = np.ones((128, 128), np.float32)
    s_idx = np.arange(128)[:, None]
    t_idx = np.arange(128)[None, :]
    causal = np.where(t_idx >= s_idx, 0.0, NEG).astype(np.float32)
    allmask = np.full((128, 128), NEG, np.float32)
    zero = np.zeros((128, 128), np.float32)
    if h == 0:
        fmask = np.concatenate([causal, allmask], axis=1)
    else:
        fmask = np.concatenate([zero, causal], axis=1)
    slopes = (2.0 ** (-8.0 * np.arange(1, 7) / 6)).astype(np.float32)
    dist_cur = (t_idx - s_idx).astype(np.float32)
    dist_prev = dist_cur + 128.0
    sb = np.zeros((3, 6, 128, 128), np.float32)
    for j in range(6):
        cur = np.where((dist_cur >= 0) & (dist_cur < 128), -slopes[j] * dist_cur, NEG)
        prev = np.where((dist_prev >= 0) & (dist_prev < 128), -slopes[j] * dist_prev, NEG)
        if h == 0:
            sb[0, j], sb[1, j], sb[2, j] = prev, cur, allmask
        else:
            sb[0, j], sb[1, j], sb[2, j] = allmask, prev, cur
    sbias = np.ascontiguousarray(sb.reshape(18, 128, 128).transpose(1, 0, 2)).reshape(128, 18 * 128)
    return ident, tri, ones, fmask.astype(np.float32), sbias.astype(np.float32)


def kernel(x, mem, ffn1_norm, ffn1_gate, ffn1_up, ffn1_down, mix_norm, mem_norm, w_in, forget_bias,
           w_mem_k, w_mem_v, fox_q_gain, fox_k_gain, swa_q_gain, swa_k_gain, swa_sinks, mem_q_gain,
           mem_k_gain, w_out, ffn2_norm, ffn2_gate, ffn2_up, ffn2_down):
    f32 = np.float32
    A = lambda a: np.ascontiguousarray(np.asarray(a, dtype=f32))
    x = A(x); mem = A(mem)
    B = x.shape[0]

    def lay_gu(g, u):
        g = A(g)[0].reshape(NCH, 128, NFB, 128).transpose(2, 1, 0, 3)
        u = A(u)[0].reshape(NCH, 128, NFB, 128).transpose(2, 1, 0, 3)
        return np.ascontiguousarray(np.stack([g, u], axis=2)).reshape(NFB, 128, 2 * NCH * 128)

    def lay_d(d):
        d = A(d)[0].reshape(NFB, 128, NCH, 128).transpose(2, 1, 0, 3)
        return np.ascontiguousarray(d).reshape(NCH, 128, NFB * 128)

    def lay_cols(w, ncol):
        g = ncol // 512
        w = w.reshape(NCH, 128, g, 512).transpose(2, 1, 0, 3)
        return np.ascontiguousarray(w).reshape(g, 128, NCH * 512)

    wgu1, wgu2 = lay_gu(ffn1_gate, ffn1_up), lay_gu(ffn2_gate, ffn2_up)
    wd1, wd2 = lay_d(ffn1_down), lay_d(ffn2_down)
    wi = A(w_in)[0]
    cuts = np.cumsum([768, 768, 768, 6, 768, 256, 256, 512])
    fq, fk, fv, fl, sq, sk, sv, mq = np.split(wi, cuts[:-1], axis=1)
    win = lay_cols(np.concatenate([fq, fk, sq, sk, mq, fv, sv], axis=1), 4096)
    wfl = np.ascontiguousarray(fl.reshape(NCH, 128, 6).transpose(1, 0, 2)).reshape(128, NCH * 6)
    wmem = np.concatenate([lay_cols(A(w_mem_k)[0], 512), lay_cols(A(w_mem_v)[0], 512)], axis=0)
    wout = lay_cols(A(w_out)[0], 2048)
    grows = np.ascontiguousarray(np.broadcast_to(
        np.stack([A(v)[0] for v in (ffn1_norm, mix_norm, mem_norm, ffn2_norm)])[:, None, :], (4, 128, D)))
    hgains = np.ascontiguousarray(np.stack([A(v)[0] for v in (fox_q_gain, fox_k_gain, swa_q_gain, swa_k_gain,
                                                               mem_q_gain, mem_k_gain)], axis=1))
    fbias = np.ascontiguousarray(np.broadcast_to(np.tile(A(forget_bias)[0], TILE_TB)[None, :], (128, TILE_TB * 6)))
    sinks = np.ascontiguousarray(np.broadcast_to(A(swa_sinks)[0][None, :], (128, 6)))

    if "nc" not in _NC_CACHE:
        import os
        _st = os.environ.get("MK_STOP")
        _NC_CACHE["nc"] = build_program(int(_st) if _st else None)
    nc = _NC_CACHE["nc"]

    in_maps = []
    for c in range(8):
        b, h = c // 2, c % 2
        xb = x[b].reshape(16, 2, 128, D)[:, h].reshape(NTOK, D)
        ident, tri, ones, fmask, sbias = _consts(h)
        in_maps.append({
            "x_loc": np.ascontiguousarray(xb), "mem_b": mem[b], "grows": grows, "hgains": hgains, "fbias": fbias,
            "sinks": sinks, "hsel": np.full((128, 1), float(h), f32), "ident": ident, "tri": tri, "ones": ones,
            "fmask": fmask, "sbias": sbias, "wgu1": wgu1, "wgu2": wgu2, "wd1": wd1, "wd2": wd2, "win": win,
            "wfl": wfl, "wmem": wmem, "wout": wout,
        })
    res = run_bass_kernel_spmd(nc, in_maps, core_ids=list(range(8)))
    out = np.empty((B, SEQ, D), f32)
    for c in range(8):
        b, h = c // 2, c % 2
        out[b].reshape(16, 2, 128, D)[:, h] = np.asarray(res.results[c]["out_loc"]).reshape(16, 128, D)
    return out
```

```python
import numpy as np
from contextlib import ExitStack
import concourse.bass as bass
import concourse.mybir as mybir
from concourse.bass_utils import run_bass_kernel_spmd

F32 = mybir.dt.float32
BF16 = mybir.dt.bfloat16
AF = mybir.ActivationFunctionType
ALU = mybir.AluOpType
AX = mybir.AxisListType

D = 2048
DFF = 5632
NFB = DFF // 128
NCH = D // 128
SEQ = 4096
NTOK = 2048
NTB = 16
TILE_TB = 8
NTILE = NTB // TILE_TB
TT = TILE_TB * 128
HD = 128
EPS = 1e-6
NEG = -30000.0
SCALE = HD ** -0.5

EPOCH = 8000
COMPUTE = ("tensor", "vector", "scalar", "gpsimd")
ENGINES = ("tensor", "vector", "scalar", "gpsimd", "sync")


class Buf:
    __slots__ = ("name", "w", "r", "dsem")

    def __init__(self, name):
        self.name = name
        self.w = None
        self.r = []
        self.dsem = None


class DmaSem:
    def __init__(self, handle):
        self.h = handle
        self.count = 0


class Op:
    __slots__ = ("eng", "fn", "deps", "inc", "dma", "dsem", "tok", "ninc")

    def __init__(self, eng, fn, dma=False, dsem=None, ninc=16):
        self.eng = eng
        self.fn = fn
        self.deps = []
        self.inc = False
        self.dma = dma
        self.dsem = dsem
        self.tok = None
        self.ninc = ninc


class Prog:
    def __init__(self, nc, stack):
        self.nc = nc
        self.stack = stack
        self.seq = []
        self.ecount = {e: 0 for e in COMPUTE}
        self.esems = {e: [] for e in COMPUTE}
        self.waited = {e: {} for e in ENGINES}
        self.last = {e: None for e in ENGINES}
        self.fence_deps = {e: [] for e in ENGINES}
        self.dma_last = {}
        self.bufs = []
        self.nsem = 0
        self.nops = 0
        self.shared_dsems = []

    def buf(self, name):
        b = Buf(name)
        self.bufs.append(b)
        return b

    def bufs_n(self, name, n):
        return [self.buf(f"{name}{i}") for i in range(n)]

    def sem(self, name):
        self.nsem += 1
        return self.stack.enter_context(self.nc.semaphore(f"{name}_{self.nsem}"))

    def dsem_for(self, buf, kind):
        if buf.dsem is None:
            buf.dsem = {}
        if kind not in buf.dsem:
            free = self.__dict__.setdefault("free_dsems", {}).setdefault(kind, [])
            buf.dsem[kind] = free.pop() if free else DmaSem(self.sem("d"))
        return buf.dsem[kind]

    def _add_dep(self, op, dep):
        if dep is None or dep is op:
            return
        if dep.tok is not None and not dep.inc and not dep.dma:
            raise RuntimeError("dependency on already-replayed non-inc op")
        if (not dep.dma) and (not op.dma) and dep.eng == op.eng and op.eng == "tensor":
            return
        dep.inc = True
        op.deps.append(dep)

    def op(self, eng, fn, reads=(), writes=(), dma=False, dbuf=None, ninc=16):
        if getattr(self, "stopped", False):
            return None
        dsem = None
        if dma:
            dsem = self.dsem_for(dbuf, eng + ("_cc" if ninc == 1 else ""))
        o = Op(eng, fn, dma=dma, dsem=dsem, ninc=ninc)
        if dma:
            o.inc = True
            self.dma_last[id(dsem)] = o
        for d in self.fence_deps[eng]:
            self._add_dep(o, d)
        self.fence_deps[eng] = []
        for b in reads:
            self._add_dep(o, b.w)
        for b in writes:
            self._add_dep(o, b.w)
            for r in b.r:
                self._add_dep(o, r)
        for b in writes:
            b.w = o
            b.r = []
        for b in reads:
            if b.w is o:
                continue
            if not dma:
                b.r = [r for r in b.r if r.dma or r.eng != eng]
            b.r.append(o)
        self.seq.append(o)
        self.last[eng] = o
        self.nops += 1
        return o

    def fence(self):
        deps = [self.last[e] for e in COMPUTE if self.last[e] is not None]
        deps += list(self.dma_last.values())
        self.dma_last = {}
        for d in deps:
            d.inc = True
        for e in ENGINES:
            self.fence_deps[e] = list(deps)
        free = self.__dict__.setdefault("free_dsems", {})
        for b in self.bufs:
            b.w = None
            b.r = []
            if b.dsem is not None:
                for kind, ds in b.dsem.items():
                    free.setdefault(kind, []).append(ds)
                b.dsem = None

    def _esem(self, eng, idx):
        while len(self.esems[eng]) <= idx:
            self.esems[eng].append(self.sem("e"))
        return self.esems[eng][idx]

    def replay(self):
        nc = self.nc
        prog = self
        per = {e: [] for e in ENGINES}
        for o in self.seq:
            if o.dma:
                o.dsem.count += o.ninc
                o.tok = (o.dsem.h, o.dsem.count)
            elif o.inc:
                c = self.ecount[o.eng]
                h = self._esem(o.eng, c // EPOCH)
                o.tok = (h, c % EPOCH + 1)
                self.ecount[o.eng] = c + 1
            else:
                o.tok = (None, 0)
            per[o.eng].append(o)
        self.seq = []

        def run(eng_name, eng):
            waited = prog.waited[eng_name]
            for o in per[eng_name]:
                need = {}
                for d in o.deps:
                    h, v = d.tok
                    k = id(h)
                    if waited.get(k, 0) >= v:
                        continue
                    if k not in need or need[k][1] < v:
                        need[k] = (h, v)
                for k, (h, v) in need.items():
                    eng.wait_ge(h, v)
                    waited[k] = v
                ins = o.fn(eng)
                if o.dma or o.inc:
                    ins.then_inc(o.tok[0], o.ninc if o.dma else 1)

        with nc.Block() as block:
            @block.tensor
            def _(e):
                run("tensor", e)

            @block.vector
            def _(e):
                run("vector", e)

            @block.scalar
            def _(e):
                run("scalar", e)

            @block.gpsimd
            def _(e):
                run("gpsimd", e)

            @block.sync
            def _(e):
                run("sync", e)

    def end_phase(self):
        self.fence()
        if not getattr(self, "single_block", False):
            self.replay()
        self.nphase = getattr(self, "nphase", 0) + 1
        if getattr(self, "stop", None) is not None and self.nphase == self.stop:
            self.stopped = True


class Ctx:
    pass


def build_program(stop=None):
    nc = bass.Bass("TRN2", target_bir_lowering=False)
    K = Ctx()
    K.nc = nc

    def din(name, shape, dt=F32):
        return nc.dram_tensor(name, list(shape), dt, kind="ExternalInput").ap()

    x_d = din("x_loc", [NTOK, D])
    mem_d = din("mem_b", [256, D])
    grow_d = din("grows", [4, 128, D])
    hg_d = din("hgains", [128, 6])
    fb_d = din("fbias", [128, TILE_TB * 6])
    sinks_d = din("sinks", [128, 6])
    hsel_d = din("hsel", [128, 1])
    ident_d = din("ident", [128, 128])
    tri_d = din("tri", [128, 128])
    ones_d = din("ones", [128, 128])
    fmask_d = din("fmask", [128, 2 * 128])
    sbias_d = din("sbias", [128, 18 * 128])
    wgu_d = [din(f"wgu{i}", [NFB, 128, 2 * NCH * 128]) for i in (1, 2)]
    wd_d = [din(f"wd{i}", [NCH, 128, NFB * 128]) for i in (1, 2)]
    win_d = din("win", [8, 128, NCH * 512])
    wfl_d = din("wfl", [128, NCH * 6])
    wmem_d = din("wmem", [2, 128, NCH * 512])
    wout_d = din("wout", [4, 128, NCH * 512])
    out_d = nc.dram_tensor("out_loc", [NTOK, D], F32, kind="ExternalOutput").ap()

    x1_d = nc.dram_tensor("x1_s", [NTOK, D], F32).ap()
    x2_d = nc.dram_tensor("x2_s", [NTOK, D], F32).ap()
    qT_d = nc.dram_tensor("qT_s", [16 * 128, NTOK], BF16).ap()
    kv_in_t = [nc.dram_tensor(f"kv_in{i}", [1024, 1024], BF16) for i in range(4)]
    kv_all_t = [nc.dram_tensor(f"kv_all{i}", [2048, 1024], BF16) for i in range(4)]
    lf_in_t = nc.dram_tensor("lf_in", [128, 96], F32)
    lf_all_t = nc.dram_tensor("lf_all", [256, 96], F32)
    cqT_t = nc.dram_tensor("cqT_s", [96, 128], F32)
    kv_in = [t.ap() for t in kv_in_t]
    kv_all = [t.ap() for t in kv_all_t]

    def kT_in(jj, ti):
        return kv_in[ti][jj * 128:(jj + 1) * 128, :]

    def kT_all(jj, r, ti):
        return kv_all[ti][r * 1024 + jj * 128: r * 1024 + (jj + 1) * 128, :]

    def v_in(ti):
        return kv_in[2 + ti]

    def v_all(ti, r):
        return kv_all[2 + ti][r * 1024:(r + 1) * 1024, :]

    RG = [[0, 1], [2, 3], [4, 5], [6, 7]]

    with ExitStack() as top:
        P = Prog(nc, top)
        P.stop = stop
        P.single_block = True
        uid = [0]

        def sb(st, name, shape, dt):
            uid[0] += 1
            return st.enter_context(nc.sbuf_tensor(f"sb{uid[0]}_{name}", list(shape), dt))

        def ps(st, name, shape, dt):
            uid[0] += 1
            return st.enter_context(nc.psum_tensor(f"ps{uid[0]}_{name}", list(shape), dt))

        def dma(eng, out, in_, reads, writes, dbuf):
            return P.op(eng, lambda e: e.dma_start(out=out, in_=in_), reads=reads, writes=writes,
                        dma=True, dbuf=dbuf)

        ident_f = sb(top, "ident_f", [128, 128], F32)
        ident_b = sb(top, "ident_b", [128, 128], BF16)
        ones_f = sb(top, "ones_f", [128, 128], F32)
        ones_b = sb(top, "ones_b", [128, 128], BF16)
        tri_f = sb(top, "tri_f", [128, 128], F32)
        hg = sb(top, "hg", [128, 6], F32)
        hgq = sb(top, "hgq", [128, 3], F32)
        fbias = sb(top, "fbias", [128, TILE_TB * 6], F32)
        esink = sb(top, "esink", [128, 6], F32)
        hsel = sb(top, "hsel", [128, 1], F32)
        lf_sb = sb(top, "lf_sb", [128, NTB * 6], F32)
        B_c = P.buf("consts")
        B_lf = P.buf("lf_sb")
        for t_sb, t_d in ((ident_f, ident_d), (ones_f, ones_d), (tri_f, tri_d),
                          (hg, hg_d), (fbias, fb_d), (esink, sinks_d), (hsel, hsel_d)):
            dma("sync", t_sb[:], t_d, [], [B_c], B_c)
        P.op("vector", lambda e: e.tensor_copy(out=ident_b[:], in_=ident_f[:]), reads=[B_c], writes=[B_c])
        P.op("vector", lambda e: e.tensor_copy(out=ones_b[:], in_=ones_f[:]), reads=[B_c], writes=[B_c])
        for i in range(3):
            P.op("vector", lambda e, i=i: e.scalar_tensor_tensor(
                out=hgq[:, i:i + 1], in0=hg[:, 2 * i:2 * i + 1], scalar=SCALE, in1=hg[:, 2 * i + 1:2 * i + 2],
                op0=ALU.mult, op1=ALU.mult), reads=[B_c], writes=[B_c])
        P.op("scalar", lambda e: e.activation(out=esink[:], in_=esink[:], func=AF.Exp), reads=[B_c], writes=[B_c])

        def norm_transpose(st, src_fn, ntb, gi, hT, tag, collect=None, npt=4):
            NX = 3
            xs = [sb(st, f"{tag}xs{i}", [128, D], F32) for i in range(NX)]
            xn = [sb(st, f"{tag}xn{i}", [128, D], BF16) for i in range(2)]
            grow = sb(st, f"{tag}grow", [128, D], F32)
            junk = sb(st, f"{tag}junk", [128, D], BF16)
            ssq = [sb(st, f"{tag}ssq{i}", [128, 1], F32) for i in range(2)]
            rstd = [sb(st, f"{tag}rstd{i}", [128, 1], F32) for i in range(2)]
            pT = [ps(st, f"{tag}pT{i}", [128, 4, 128], BF16) for i in range(npt)]
            Bxn, Bss, Brs = (P.bufs_n(tag + n, 2) for n in ("xn", "ss", "rs"))
            Bxs = P.bufs_n(tag + "xs", NX)
            BpT = P.bufs_n(tag + "pT", npt)
            Bjunk, Bgrow = P.buf(tag + "junk"), P.buf(tag + "grow")
            dma("sync", grow[:], grow_d[gi], [], [Bgrow], Bgrow)
            kc = [0]

            def stage1(tb):
                i = tb % 2
                x_ = xs[tb % NX]
                Bx = Bxs[tb % NX]
                dma("sync", x_[:], src_fn(tb), [], [Bx], Bx)
                P.op("scalar", lambda e: e.activation(out=junk[:], in_=x_[:], func=AF.Square, accum_out=ssq[i][:]),
                     reads=[Bx], writes=[Bjunk, Bss[i]])
                P.op("scalar", lambda e: e.activation(out=rstd[i][:], in_=ssq[i][:], func=AF.Ln, bias=EPS, scale=1.0 / D),
                     reads=[Bss[i]], writes=[Brs[i]])
                P.op("scalar", lambda e: e.activation(out=rstd[i][:], in_=rstd[i][:], func=AF.Exp, scale=-0.5),
                     reads=[Brs[i]], writes=[Brs[i]])
                P.op("vector", lambda e: e.scalar_tensor_tensor(
                    out=xn[i][:], in0=x_[:], scalar=rstd[i][:, 0:1], in1=grow[:], op0=ALU.mult, op1=ALU.mult),
                    reads=[Bx, Brs[i], Bgrow], writes=[Bxn[i]])

            def stage2(tb):
                i = tb % 2
                for cg in range(4):
                    kk = kc[0] % npt
                    kc[0] += 1
                    for cc in range(4):
                        c = cg * 4 + cc
                        P.op("tensor", lambda e, c=c, cc=cc, kk=kk: e.transpose(
                            out=pT[kk][:, cc, :], in_=xn[i][:, c * 128:(c + 1) * 128], identity=ident_b[:]),
                            reads=[Bxn[i], B_c], writes=[BpT[kk]])
                    dst = hT[:, cg * 4:(cg + 1) * 4, tb * 128:(tb + 1) * 128]
                    Bd = P.buf(tag + "hTp")
                    if collect is not None:
                        collect[tb].append(Bd)
                    if cg % 2 == 0:
                        P.op("scalar", lambda e, dst=dst, kk=kk: e.copy(out=dst, in_=pT[kk][:]),
                             reads=[BpT[kk]], writes=[Bd])
                    else:
                        P.op("vector", lambda e, dst=dst, kk=kk: e.tensor_copy(out=dst, in_=pT[kk][:]),
                             reads=[BpT[kk]], writes=[Bd])

            for tb in range(ntb):
                stage1(tb)
                if tb >= 1:
                    stage2(tb - 1)
            stage2(ntb - 1)

        def ffn_tile(src_rows, dst_rows, row0, gi, wgu, wd, tag, hooks=None):
            with ExitStack() as s_at:
                AT = sb(s_at, tag + "AT", [128, NFB, TT], BF16)
                with ExitStack() as s_h:
                    hT = sb(s_h, tag + "hT", [128, NCH, TT], BF16)
                    with ExitStack() as s2:
                        BhTtb = [[] for _ in range(TILE_TB)]
                        norm_transpose(s2, lambda tb: src_rows[row0 + tb * 128: row0 + (tb + 1) * 128, :],
                                       TILE_TB, gi, hT, tag + "n", collect=BhTtb)
                        BhT_half = [[b for tb in range(4 * hf, 4 * hf + 4) for b in BhTtb[tb]] for hf in range(2)]
                        NS = 3
                        wsl = [sb(s2, f"{tag}wgu{i}", [128, 2, NCH, 128], BF16) for i in range(NS)]
                        sg = [sb(s2, f"{tag}sg{i}", [128, 512], F32) for i in range(2)]
                        pg = [ps(s2, f"{tag}pg{i}", [128, 512], F32) for i in range(2)]
                        pu = [ps(s2, f"{tag}pu{i}", [128, 512], F32) for i in range(2)]
                        Bw = P.bufs_n(tag + "wgu", NS)
                        Bsg, Bpg, Bpu = (P.bufs_n(tag + n, 2) for n in ("sg", "pg", "pu"))
                        k = 0
                        for fb in range(NFB):
                            sl = fb % NS
                            if hooks and fb in hooks:
                                hooks[fb]()
                            dma("gpsimd", wsl[sl][:],
                                wgu[fb].rearrange("p (a c j) -> p a c j", a=2, c=NCH), [], [Bw[sl]], Bw[sl])
                            for half in range(2):
                                kk = k % 2
                                k += 1
                                rhs_sl = slice(half * 512, (half + 1) * 512)
                                for a, pp_, Bp in ((0, pg, Bpg), (1, pu, Bpu)):
                                    for c in range(NCH):
                                        P.op("tensor", lambda e, a=a, c=c, sl=sl, kk=kk, pp_=pp_, rhs_sl=rhs_sl: e.matmul(
                                            pp_[kk][:], lhsT=wsl[sl][:, a, c, :], rhs=hT[:, c, rhs_sl],
                                            start=(c == 0), stop=(c == NCH - 1)),
                                            reads=[Bw[sl]] + BhT_half[half], writes=[Bp[kk]])
                                P.op("scalar", lambda e, kk=kk: e.activation(out=sg[kk][:], in_=pg[kk][:], func=AF.Silu),
                                     reads=[Bpg[kk]], writes=[Bsg[kk]])
                                P.op("vector", lambda e, kk=kk, fb=fb, rhs_sl=rhs_sl: e.tensor_tensor(
                                    out=AT[:, fb, rhs_sl], in0=sg[kk][:], in1=pu[kk][:], op=ALU.mult),
                                    reads=[Bsg[kk], Bpu[kk]], writes=[P.buf(tag + "ATp")])
                        P.end_phase()
                with ExitStack() as s3:
                    NS = 3
                    wdl = [sb(s3, f"{tag}wd{i}", [128, NFB, 128], BF16) for i in range(NS)]
                    xsp = [sb(s3, f"{tag}xsp{i}", [128, TILE_TB, 512], F32) for i in range(2)]
                    yT = [sb(s3, f"{tag}yT{i}", [128, 512], F32) for i in range(3)]
                    py = [ps(s3, f"{tag}py{i}", [128, 512], F32) for i in range(3)]
                    pyt = [ps(s3, f"{tag}pyt{i}", [128, 4, 128], F32) for i in range(2)]
                    Bw = P.bufs_n(tag + "wd", NS)
                    Bxsp, Bpyt = (P.bufs_n(tag + n, 2) for n in ("xsp", "pyt"))
                    ByT, Bpy = (P.bufs_n(tag + n, 3) for n in ("yT", "py"))
                    BAT = P.buf(tag + "ATr")
                    pending = []

                    def finish(item):
                        k_, sd_, half_, dbl_, last_ = item
                        k3, k2 = k_ % 3, k_ % 2
                        for tt in range(4):
                            P.op("tensor", lambda e, k3=k3, k2=k2, tt=tt: e.transpose(
                                out=pyt[k2][:, tt, :], in_=yT[k3][:, tt * 128:(tt + 1) * 128], identity=ident_f[:]),
                                reads=[ByT[k3]], writes=[Bpyt[k2]])
                        dst = xsp[sd_][:, half_ * 4:(half_ + 1) * 4, dbl_ * 128:(dbl_ + 1) * 128]
                        P.op("vector", lambda e, k2=k2, dst=dst: e.scalar_tensor_tensor(
                            out=dst, in0=pyt[k2][:], scalar=0.5, in1=dst, op0=ALU.mult, op1=ALU.add),
                            reads=[Bpyt[k2], Bxsp[sd_]], writes=[Bxsp[sd_]])
                        if last_ is not None:
                            dma("sync", last_, xsp[sd_][:], [Bxsp[sd_]], [], Bxsp[sd_])

                    k = 0
                    for dg in range(4):
                        sd = dg % 2
                        csl = slice(dg * 512, (dg + 1) * 512)
                        dma("sync", xsp[sd][:],
                            src_rows[row0:row0 + TT, csl].rearrange("(tb p) c -> p tb c", p=128),
                            [], [Bxsp[sd]], Bxsp[sd])
                        for dbl in range(4):
                            db = dg * 4 + dbl
                            sl = db % NS
                            dma("gpsimd", wdl[sl][:], wd[db].rearrange("p (f j) -> p f j", f=NFB),
                                [], [Bw[sl]], Bw[sl])
                            for half in range(2):
                                k3 = k % 3
                                for fc in range(NFB):
                                    P.op("tensor", lambda e, fc=fc, sl=sl, k3=k3, half=half: e.matmul(
                                        py[k3][:], lhsT=wdl[sl][:, fc, :], rhs=AT[:, fc, half * 512:(half + 1) * 512],
                                        start=(fc == 0), stop=(fc == NFB - 1)),
                                        reads=[Bw[sl], BAT], writes=[Bpy[k3]])
                                P.op("scalar", lambda e, k3=k3: e.copy(out=yT[k3][:], in_=py[k3][:]),
                                     reads=[Bpy[k3]], writes=[ByT[k3]])
                                last = None
                                if dbl == 3 and half == 1:
                                    last = dst_rows[row0:row0 + TT, csl].rearrange("(tb p) c -> p tb c", p=128)
                                pending.append((k, sd, half, dbl, last))
                                k += 1
                                if len(pending) > 1:
                                    finish(pending.pop(0))
                    while pending:
                        finish(pending.pop(0))
                    P.end_phase()

        def qk_heads(st, hT, BhT_fn, ncols, heads, tag, npq=3):
            pq = [ps(st, f"{tag}pq{i}", [128, 512], F32) for i in range(npq)]
            pss = [ps(st, f"{tag}pss{i}", [128, 512], F32) for i in range(2)]
            qs = [sb(st, f"{tag}qs{i}", [128, 512], F32) for i in range(2)]
            sq = [sb(st, f"{tag}sq{i}", [128, 512], F32) for i in range(2)]
            rst = [sb(st, f"{tag}rst{i}", [128, 512], F32) for i in range(2)]
            qo = [sb(st, f"{tag}qo{i}", [128, 512], BF16) for i in range(3)]
            Bpq, Bqo = P.bufs_n(tag + "pq", npq), P.bufs_n(tag + "qo", 3)
            Bpss, Bsq, Brst, Bqs = (P.bufs_n(tag + n, 2) for n in ("pss", "sq", "rst", "qs"))
            pieces = [(c0, min(512, ncols - c0)) for c0 in range(0, ncols, 512)]
            pend = []

            def finish(item):
                k_, n_, gain_, dst_ = item
                k3, k2 = k_ % 3, k_ % 2
                P.op("tensor", lambda e: e.matmul(pss[k2][:, 0:n_], lhsT=ones_f[:], rhs=sq[k2][:, 0:n_], start=True, stop=True),
                     reads=[Bsq[k2], B_c], writes=[Bpss[k2]])
                P.op("scalar", lambda e: e.activation(out=rst[k2][:, 0:n_], in_=pss[k2][:, 0:n_], func=AF.Ln,
                                                      bias=EPS, scale=1.0 / HD),
                     reads=[Bpss[k2]], writes=[Brst[k2]])
                P.op("scalar", lambda e: e.activation(out=rst[k2][:, 0:n_], in_=rst[k2][:, 0:n_], func=AF.Exp, scale=-0.5),
                     reads=[Brst[k2]], writes=[Brst[k2]])
                dst, is_sbuf, Bd = dst_
                out_ap = dst if is_sbuf else qo[k3][:, 0:n_]
                wr = [Bd] if is_sbuf else [Bqo[k3]]
                if gain_ is not None:
                    P.op("vector", lambda e: e.scalar_tensor_tensor(
                        out=out_ap, in0=qs[k2][:, 0:n_], scalar=gain_, in1=rst[k2][:, 0:n_], op0=ALU.mult, op1=ALU.mult),
                        reads=[Bqs[k2], Brst[k2], B_c], writes=wr)
                else:
                    P.op("vector", lambda e: e.tensor_tensor(out=out_ap, in0=qs[k2][:, 0:n_], in1=rst[k2][:, 0:n_], op=ALU.mult),
                         reads=[Bqs[k2], Brst[k2]], writes=wr)
                if not is_sbuf:
                    dma("sync", dst, qo[k3][:, 0:n_], [Bqo[k3]], [], Bqo[k3])

            k = 0
            for (pre_fn, w_ap_fn, Bw, gain, dst_fn) in heads:
                if pre_fn is not None:
                    pre_fn()
                for (c0, n) in pieces:
                    kq_, k2 = k % npq, k % 2
                    for c in range(NCH):
                        P.op("tensor", lambda e, c=c, kq_=kq_, c0=c0, n=n, w_ap_fn=w_ap_fn: e.matmul(
                            pq[kq_][:, 0:n], lhsT=w_ap_fn(c), rhs=hT[:, c, c0:c0 + n], start=(c == 0), stop=(c == NCH - 1)),
                            reads=[Bw] + BhT_fn(c0, n), writes=[Bpq[kq_]])
                    P.op("scalar", lambda e, kq_=kq_, k2=k2, n=n: e.copy(out=qs[k2][:, 0:n], in_=pq[kq_][:, 0:n]),
                         reads=[Bpq[kq_]], writes=[Bqs[k2]])
                    P.op("vector", lambda e, k2=k2, n=n: e.tensor_tensor(out=sq[k2][:, 0:n], in0=qs[k2][:, 0:n],
                                                                         in1=qs[k2][:, 0:n], op=ALU.mult),
                         reads=[Bqs[k2]], writes=[Bsq[k2]])
                    pend.append((k, n, gain, dst_fn(c0, n)))
                    k += 1
                    if len(pend) > 1:
                        finish(pend.pop(0))
            while pend:
                finish(pend.pop(0))
            return pq, Bpq

        B_g1 = P.buf("g1")

        def exchange(chunks):
            for i in chunks:
                P.op("gpsimd", lambda e, i=i: e.collective_compute("AllGather", ALU.bypass, replica_groups=RG,
                                                                   ins=[kv_in_t[i].ap().opt()], outs=[kv_all_t[i].ap().opt()]),
                     writes=[B_g1], dma=True, dbuf=B_g1, ninc=1)

        for ti in range(NTILE):
            row0 = ti * TT
            hooks = {3: (lambda: exchange((0,))), 9: (lambda: exchange((2,)))} if ti == 1 else None
            ffn_tile(x_d, x1_d, row0, 0, wgu_d[0], wd_d[0], f"f1t{ti}", hooks=hooks)
            def proj_tile(ti, row0):
                with ExitStack() as s_h:
                    h2T = sb(s_h, f"p{ti}h2T", [128, NCH, TT], BF16)
                    with ExitStack() as s2:
                        tag = f"p{ti}"
                        win = [sb(s2, f"{tag}win{i}", [128, NCH, 512], BF16) for i in range(2)]
                        Bwin = P.bufs_n(tag + "win", 2)
                        Bh2tb = [[] for _ in range(TILE_TB)]
                        norm_transpose(s2, lambda tb: x1_d[row0 + tb * 128: row0 + (tb + 1) * 128, :],
                                       TILE_TB, 1, h2T, f"p{ti}n", collect=Bh2tb, npt=2)
                        Bh2_fn = lambda c0, n: [b for tb in range(c0 // 128, (c0 + n) // 128) for b in Bh2tb[tb]]

                        def load_group(g):
                            sl = g % 2
                            dma("gpsimd", win[sl][:], win_d[g].rearrange("p (c j) -> p c j", c=NCH), [], [Bwin[sl]], Bwin[sl])

                        heads = []
                        load_group(0)
                        for hidx in range(24):
                            g, hh = hidx // 4, hidx % 4
                            sl = g % 2
                            if hidx < 6:
                                gain, drow = hgq[:, 0:1], qT_d[hidx * 128:(hidx + 1) * 128, row0:row0 + TT]
                            elif hidx < 12:
                                gain, drow = None, kT_in(hidx - 6, ti)
                            elif hidx < 18:
                                gain, drow = hgq[:, 1:2], qT_d[(hidx - 6) * 128:(hidx - 5) * 128, row0:row0 + TT]
                            elif hidx < 20:
                                gain, drow = None, kT_in(hidx - 12, ti)
                            else:
                                gain, drow = hgq[:, 2:3], qT_d[(hidx - 8) * 128:(hidx - 7) * 128, row0:row0 + TT]
                            heads.append((
                                (lambda g=g: load_group(g + 1)) if hh == 0 else None,
                                (lambda c, sl=sl, hh=hh: win[sl][:, c, hh * 128:(hh + 1) * 128]),
                                Bwin[sl], gain,
                                (lambda c0, n, drow=drow: (drow[:, c0:c0 + n], False, None))))
                        pq_, Bpq_ = qk_heads(s2, h2T, Bh2_fn, TT, heads, tag + "q")
                        vst = [sb(s2, f"{tag}vst{i}", [128, TILE_TB, 512], BF16) for i in range(2)]
                        wfl = sb(s2, tag + "wfl", [128, NCH, 6], BF16)
                        zt = sb(s2, tag + "zt", [128, TILE_TB, 6], F32)
                        pp, Bpp = pq_, Bpq_
                        pz = ps(s2, tag + "pz", [128, TILE_TB, 8], F32)
                        Bvst = P.bufs_n(tag + "vst", 2)
                        Bwfl, Bpz, Bzt = P.buf(tag + "wfl"), P.buf(tag + "pz"), P.buf(tag + "zt")
                        dma("gpsimd", wfl[:], wfl_d.rearrange("p (c j) -> p c j", c=NCH), [], [Bwfl], Bwfl)
                        k = 0
                        for g in (6, 7):
                            sl = g % 2
                            if g == 6:
                                load_group(7)
                            for tb in range(TILE_TB):
                                kk = k % 2
                                k += 1
                                tsl = slice(tb * 128, (tb + 1) * 128)
                                for c in range(NCH):
                                    P.op("tensor", lambda e, c=c, sl=sl, kk=kk, tsl=tsl: e.matmul(
                                        pp[kk][:], lhsT=h2T[:, c, tsl], rhs=win[sl][:, c, :], start=(c == 0), stop=(c == NCH - 1)),
                                        reads=[Bwin[sl]] + Bh2tb[tb], writes=[Bpp[kk]])
                                if g == 6:
                                    for c in range(NCH):
                                        P.op("tensor", lambda e, c=c, tb=tb, tsl=tsl: e.matmul(
                                            pz[:, tb, 0:6], lhsT=h2T[:, c, tsl], rhs=wfl[:, c, :], start=(c == 0), stop=(c == NCH - 1)),
                                            reads=[Bwfl] + Bh2tb[tb], writes=[Bpz])
                                if tb % 2 == 0:
                                    P.op("scalar", lambda e, kk=kk, sl=sl, tb=tb: e.copy(out=vst[sl][:, tb, :], in_=pp[kk][:]),
                                         reads=[Bpp[kk]], writes=[Bvst[sl]])
                                else:
                                    P.op("vector", lambda e, kk=kk, sl=sl, tb=tb: e.tensor_copy(out=vst[sl][:, tb, :], in_=pp[kk][:]),
                                         reads=[Bpp[kk]], writes=[Bvst[sl]])
                            dma("sync", v_in(ti)[:, (g - 6) * 512:(g - 5) * 512].rearrange("(tb p) c -> p tb c", p=128),
                                vst[sl][:], [Bvst[sl]], [], Bvst[sl])
                        P.op("vector", lambda e: e.tensor_tensor(out=zt[:], in0=pz[:, :, 0:6],
                                                                 in1=fbias[:].rearrange("p (t j) -> p t j", j=6), op=ALU.add),
                             reads=[Bpz], writes=[Bzt])
                        P.op("scalar", lambda e: e.activation(out=zt[:], in_=zt[:], func=AF.Exp, scale=-1.0), reads=[Bzt], writes=[Bzt])
                        P.op("scalar", lambda e: e.activation(out=zt[:], in_=zt[:], func=AF.Ln, bias=1.0), reads=[Bzt], writes=[Bzt])
                        P.op("vector", lambda e, ti=ti: e.tensor_scalar(
                            out=lf_sb[:, ti * TILE_TB * 6:(ti + 1) * TILE_TB * 6].rearrange("p (t j) -> p t j", j=6),
                            in0=zt[:], scalar1=-1.0, scalar2=None, op0=ALU.mult),
                            reads=[Bzt], writes=[B_lf])
                        P.end_phase()

            proj_tile(ti, row0)
        mkT = sb(top, "mkT", [128, 4, 256], BF16)
        mv = sb(top, "mv", [128, 2, 512], BF16)
        def phase_b():
            with ExitStack() as sB:
                B_lfin = P.buf("lfin")
                B_g2 = P.buf("g2")
                mnT = sb(sB, "mnT", [128, NCH, 256], BF16)
                wm = [sb(sB, f"wm{i}", [128, NCH, 512], BF16) for i in range(2)]
                pp = [ps(sB, f"mpp{i}", [128, 512], F32) for i in range(2)]
                Bwm, Bpp = P.bufs_n("wm", 2), P.bufs_n("mpp", 2)
                BmkT, Bmv = P.buf("mkT"), P.buf("mv")
                dma("sync", lf_in_t.ap(), lf_sb[:], [B_lf], [B_lfin], B_lfin)
                for i in range(2):
                    dma("gpsimd", wm[i][:], wmem_d[i].rearrange("p (c j) -> p c j", c=NCH), [], [Bwm[i]], Bwm[i])
                exchange((1, 3))
                P.op("gpsimd", lambda e: e.collective_compute("AllGather", ALU.bypass, replica_groups=RG,
                                                              ins=[lf_in_t.ap().opt()], outs=[lf_all_t.ap().opt()]),
                     reads=[B_lfin], writes=[B_g2], dma=True, dbuf=B_g2, ninc=1)
                BmnTtb = [[], []]
                norm_transpose(sB, lambda tb: mem_d[tb * 128:(tb + 1) * 128, :], 2, 2, mnT, "mn", collect=BmnTtb, npt=2)
                BmnT = BmnTtb[0] + BmnTtb[1]
                heads = []
                for hh in range(4):
                    heads.append((None, (lambda c, hh=hh: wm[0][:, c, hh * 128:(hh + 1) * 128]), Bwm[0], None,
                                  (lambda c0, n, hh=hh: (mkT[:, hh, c0:c0 + n], True, BmkT))))
                qk_heads(sB, mnT, (lambda c0, n: BmnT), 256, heads, "mk", npq=2)
                for tb in range(2):
                    for c in range(NCH):
                        P.op("tensor", lambda e, c=c, tb=tb: e.matmul(
                            pp[tb][:], lhsT=mnT[:, c, tb * 128:(tb + 1) * 128], rhs=wm[1][:, c, :],
                            start=(c == 0), stop=(c == NCH - 1)),
                            reads=[Bwm[1]] + BmnT, writes=[Bpp[tb]])
                    P.op("vector", lambda e, tb=tb: e.tensor_copy(out=mv[:, tb, :], in_=pp[tb][:]),
                         reads=[Bpp[tb]], writes=[Bmv])
                P.end_phase()

        phase_b()

        with ExitStack() as s_c:
            mixT = sb(s_c, "mixT", [128, 16, NTOK], BF16)
            negck = sb(s_c, "negck", [128, 32 * 6], F32)
            cq = sb(s_c, "cq", [128, NTB * 6], F32)
            fmask = sb(s_c, "fmask", [128, 2, 128], F32)
            fmask_b = sb(s_c, "fmask_b", [128, 2, 128], BF16)
            Bcum = P.buf("cum")
            with ExitStack() as s_vf:
                vf = sb(s_vf, "vf", [128, 2, 16, 768], BF16)
                Bvf = P.buf("vf")
                kT = [sb(s_vf, f"kT{i}", [128, 2, NTOK], BF16) for i in range(2)]
                qT = [sb(s_vf, f"qT{i}", [128, NTOK], BF16) for i in range(2)]
                BkT, BqT = P.bufs_n("kT", 2), P.bufs_n("qT", 2)

                def load_kq(j):
                    hb = j % 2
                    for r in range(2):
                        for tih in range(2):
                            dma("sync", kT[hb][:, r, tih * TT:(tih + 1) * TT], kT_all(j, r, tih), [], [BkT[hb]], BkT[hb])
                    dma("sync", qT[hb][:], qT_d[j * 128:(j + 1) * 128, :], [], [BqT[hb]], BqT[hb])

                def phase_c0():
                    with ExitStack() as s0:
                        lf = sb(s0, "lf", [128, 32, 6], F32)
                        tot = sb(s0, "tot", [128, 32, 6], F32)
                        pre = sb(s0, "pre", [128, 32, 6], F32)
                        ck = sb(s0, "ck", [128, 32, 6], F32)
                        dlt = sb(s0, "dlt", [128, 16, 6], F32)
                        pc = ps(s0, "pc", [128, 192], F32)
                        ptot = ps(s0, "ptot", [128, 192], F32)
                        Blfl, Btot, Bpre, Bpc, Bptot = (P.buf(n) for n in ("lfl", "tot", "pre", "pc", "ptot"))
                        dma("sync", fmask[:], fmask_d.rearrange("p (a t) -> p a t", a=2), [], [Bcum], Bcum)
                        P.op("vector", lambda e: e.tensor_copy(out=fmask_b[:], in_=fmask[:]), reads=[Bcum], writes=[Bcum])
                        lfv = lf[:].rearrange("p (pl r) j -> p pl r j", r=2)
                        for r in range(2):
                            dma("sync", lfv[:, :, r, :], lf_all_t.ap()[r * 128:(r + 1) * 128, :].rearrange("p (pl j) -> p pl j", j=6),
                                [], [Blfl], Blfl)
                        for r in range(2):
                            for tih in range(2):
                                dma("sync", vf[:, r, tih * 8:(tih + 1) * 8, :],
                                    v_all(tih, r)[:, 0:768].rearrange("(pl p) c -> p pl c", p=128), [], [Bvf], Bvf)
                        load_kq(0)
                        lf2 = lf[:].rearrange("p g j -> p (g j)")
                        P.op("tensor", lambda e: e.matmul(pc[:], lhsT=tri_f[:], rhs=lf2, start=True, stop=True),
                             reads=[Blfl], writes=[Bpc])
                        P.op("tensor", lambda e: e.matmul(ptot[:], lhsT=ones_f[:], rhs=lf2, start=True, stop=True),
                             reads=[Blfl], writes=[Bptot])
                        P.op("vector", lambda e: e.tensor_copy(out=tot[:].rearrange("p g j -> p (g j)"), in_=ptot[:]),
                             reads=[Bptot], writes=[Btot])
                        scan = [tot, sb(s0, "scan1", [128, 32, 6], F32)]
                        cur = 0
                        for stp in (1, 2, 4, 8, 16):
                            src_, dst_ = scan[cur], scan[1 - cur]
                            P.op("vector", lambda e, src_=src_, dst_=dst_, stp=stp: e.tensor_tensor(
                                out=dst_[:, stp:32, :], in0=src_[:, stp:32, :], in1=src_[:, 0:32 - stp, :], op=ALU.add),
                                reads=[Btot], writes=[Btot])
                            P.op("vector", lambda e, src_=src_, dst_=dst_, stp=stp: e.tensor_copy(
                                out=dst_[:, 0:stp, :], in_=src_[:, 0:stp, :]), reads=[Btot], writes=[Btot])
                            cur = 1 - cur
                        incl = scan[cur]
                        P.op("vector", lambda e: e.memset(pre[:, 0, :], 0.0), writes=[Bpre])
                        P.op("vector", lambda e: e.tensor_copy(out=pre[:, 1:32, :], in_=incl[:, 0:31, :]), reads=[Btot], writes=[Bpre])
                        P.op("vector", lambda e: e.tensor_tensor(out=ck[:].rearrange("p g j -> p (g j)"), in0=pc[:],
                                                                 in1=pre[:].rearrange("p g j -> p (g j)"), op=ALU.add),
                             reads=[Bpc, Bpre], writes=[Bcum])
                        P.op("vector", lambda e: e.tensor_scalar(out=negck[:], in0=ck[:].rearrange("p g j -> p (g j)"),
                                                                 scalar1=-1.0, scalar2=None, op0=ALU.mult),
                             reads=[Bcum], writes=[Bcum])
                        ckv = ck[:].rearrange("p (pl r) j -> p pl r j", r=2)
                        P.op("vector", lambda e: e.tensor_tensor(out=dlt[:], in0=ckv[:, :, 1, :], in1=ckv[:, :, 0, :], op=ALU.subtract),
                             reads=[Bcum], writes=[Bcum])
                        P.op("vector", lambda e: e.scalar_tensor_tensor(
                            out=cq[:].rearrange("p (pl j) -> p pl j", j=6), in0=dlt[:], scalar=hsel[:, 0:1],
                            in1=ckv[:, :, 0, :], op0=ALU.mult, op1=ALU.add),
                            reads=[Bcum], writes=[Bcum])
                        pcq = ps(s0, "pcq", [96, 128], F32)
                        cqT = sb(s0, "cqT", [96, 128], F32)
                        Bpcq, BcqT = P.buf("pcq"), P.buf("cqT")
                        P.op("tensor", lambda e: e.transpose(out=pcq[:], in_=cq[:], identity=ident_f[:]), reads=[Bcum], writes=[Bpcq])
                        P.op("vector", lambda e: e.tensor_copy(out=cqT[:], in_=pcq[:]), reads=[Bpcq], writes=[BcqT])
                        dma("sync", cqT_t.ap(), cqT[:], [BcqT], [], BcqT)
                        P.end_phase()

                phase_c0()

                def attn_epilogue(po_ap, pl_ap, dst_ap, rl_ap, Bpo, Bpl, Brl, extra=None):
                    if extra is not None:
                        P.op("scalar", lambda e: e.activation(out=rl_ap, in_=pl_ap, func=AF.Ln, bias=extra),
                             reads=[Bpl], writes=[Brl])
                    else:
                        P.op("scalar", lambda e: e.activation(out=rl_ap, in_=pl_ap, func=AF.Ln), reads=[Bpl], writes=[Brl])
                    P.op("scalar", lambda e: e.activation(out=rl_ap, in_=rl_ap, func=AF.Exp, scale=-1.0), reads=[Brl], writes=[Brl])
                    P.op("vector", lambda e: e.tensor_tensor(out=dst_ap, in0=po_ap, in1=rl_ap, op=ALU.mult),
                         reads=[Bpo, Brl], writes=[P.buf("mixTp")])

                def phase_c1():
                    with ExitStack() as s1:
                        CT = [sb(s1, f"CT{i}", [128, NTOK], F32) for i in range(2)]
                        NB = 4
                        X = [sb(s1, f"X{i}", [128, 512], F32) for i in range(NB)]
                        PT = [sb(s1, f"PT{i}", [128, 512], BF16) for i in range(NB)]
                        rl = [sb(s1, f"rl{i}", [128, 512], F32) for i in range(2)]
                        pS = [ps(s1, f"pS{i}", [128, 512], F32) for i in range(NB)]
                        pO = [ps(s1, f"pO{i}", [128, 512], F32) for i in range(2)]
                        pL = [ps(s1, f"pL{i}", [128, 512], F32) for i in range(2)]
                        BCT, Brl, BpO, BpL = (P.bufs_n(n, 2) for n in ("CT", "rl", "pO", "pL"))
                        BX, BPT, BpS = (P.bufs_n(n, NB) for n in ("X", "PT", "pS"))
                        def load_head(j, kq=True):
                            hb = j % 2
                            if kq:
                                load_kq(j)
                            dma("sync", CT[hb][:].rearrange("p (a b) -> p a b", a=NTB),
                                bass.AP(cqT_t, j * 128, [[0, 128], [6 * 128, NTB], [1, 128]]), [], [BCT[hb]], BCT[hb])

                        load_head(0, kq=False)
                        kq = 0
                        pend = []
                        for j in range(6):
                            hb = j % 2
                            if j + 1 < 6:
                                load_head(j + 1)
                            tiles = []
                            for qc in range(4):
                                ng = 8 * qc + 8
                                for g in range(ng):
                                    tiles.append((qc, g, ng))

                            def stage_a(qc, g, ng, kk, hb=hb, j=j):
                                r, pl_ = g % 2, g // 2
                                lo = max(pl_ - 4 * qc, 0) * 128
                                qsl = slice(qc * 512 + lo, (qc + 1) * 512)
                                diag = pl_ >= 4 * qc
                                P.op("tensor", lambda e: e.matmul(
                                    pS[kk][:, lo:512], lhsT=kT[hb][:, r, pl_ * 128:(pl_ + 1) * 128], rhs=qT[hb][:, qsl],
                                    start=True, stop=not diag),
                                    reads=[BkT[hb], BqT[hb]], writes=[BpS[kk]])
                                if diag:
                                    P.op("tensor", lambda e: e.matmul(
                                        pS[kk][:, lo:lo + 128], lhsT=ident_b[:], rhs=fmask_b[:, r, :], start=False, stop=True),
                                        reads=[Bcum], writes=[BpS[kk]])
                                P.op("vector", lambda e: e.tensor_tensor(
                                    out=X[kk][:, lo:512], in0=pS[kk][:, lo:512], in1=CT[hb][:, qsl], op=ALU.add),
                                    reads=[BpS[kk], BCT[hb]], writes=[BX[kk]])
                                P.op("scalar", lambda e: e.activation(
                                    out=PT[kk][:, lo:512], in_=X[kk][:, lo:512], func=AF.Exp,
                                    bias=negck[:, g * 6 + j:g * 6 + j + 1]),
                                    reads=[BX[kk], Bcum], writes=[BPT[kk]])

                            def stage_b(qc, g, ng, kk, j=j):
                                r, pl_ = g % 2, g // 2
                                lo = max(pl_ - 4 * qc, 0) * 128
                                ob = (j * 4 + qc) % 2
                                P.op("tensor", lambda e: e.matmul(
                                    pO[ob][:, lo:512], lhsT=vf[:, r, pl_, j * 128:(j + 1) * 128], rhs=PT[kk][:, lo:512],
                                    start=(g == 0), stop=(g == ng - 1)),
                                    reads=[Bvf, BPT[kk]], writes=[BpO[ob]])
                                P.op("tensor", lambda e: e.matmul(
                                    pL[ob][:, lo:512], lhsT=ones_b[:], rhs=PT[kk][:, lo:512],
                                    start=(g == 0), stop=(g == ng - 1)),
                                    reads=[BPT[kk]], writes=[BpL[ob]])
                                if g == ng - 1:
                                    attn_epilogue(pO[ob][:], pL[ob][:], mixT[:, j, qc * 512:(qc + 1) * 512], rl[ob][:],
                                                  BpO[ob], BpL[ob], Brl[ob])

                            for ti_, (qc, g, ng) in enumerate(tiles):
                                kk = kq % NB
                                kq += 1
                                stage_a(qc, g, ng, kk)
                                pend.append((stage_b, (qc, g, ng, kk)))
                                if len(pend) > NB - 1:
                                    fn_, a_ = pend.pop(0)
                                    fn_(*a_)
                        while pend:
                            fn_, a_ = pend.pop(0)
                            fn_(*a_)
                        P.end_phase()

                phase_c1()

            def phase_c2():
                with ExitStack() as s2:
                    vs = sb(s2, "vs", [128, 2, 16, 256], BF16)
                    skT = sb(s2, "skT", [128, 2, 2, NTOK], BF16)
                    qT = [sb(s2, f"sqT{i}", [128, NTOK], BF16) for i in range(2)]
                    sbias = sb(s2, "sbias", [128, 18, 512], F32)
                    sbias1 = sb(s2, "sbias1", [128, 18, 128], F32)
                    NB = 4
                    X = [sb(s2, f"sX{i}", [128, 512], F32) for i in range(NB)]
                    PT = [sb(s2, f"sPT{i}", [128, 512], BF16) for i in range(NB)]
                    rl = [sb(s2, f"srl{i}", [128, 512], F32) for i in range(2)]
                    pS = [ps(s2, f"spS{i}", [128, 512], F32) for i in range(NB)]
                    pO = [ps(s2, f"spO{i}", [128, 512], F32) for i in range(2)]
                    pL = [ps(s2, f"spL{i}", [128, 512], F32) for i in range(2)]
                    Bvs, BskT, Bsb = P.buf("vs"), P.buf("skT"), P.buf("sbias")
                    BqT, Brl, BpO, BpL = (P.bufs_n("s" + n, 2) for n in ("qT", "rl", "pO", "pL"))
                    BX, BPT, BpS = (P.bufs_n("s" + n, NB) for n in ("X", "PT", "pS"))
                    Bsb1 = P.buf("sbias1")
                    dma("sync", qT[0][:], qT_d[6 * 128:7 * 128, :], [], [BqT[0]], BqT[0])
                    for r in range(2):
                        for kvh in range(2):
                            for tih in range(2):
                                dma("sync", skT[:, kvh, r, tih * TT:(tih + 1) * TT], kT_all(6 + kvh, r, tih), [], [BskT], BskT)
                    dma("sync", sbias1[:], sbias_d.rearrange("p (a t) -> p a t", a=18), [], [Bsb1], Bsb1)
                    for pi in range(4):
                        P.op("gpsimd", lambda e, pi=pi: e.tensor_copy(out=sbias[:, :, pi * 128:(pi + 1) * 128], in_=sbias1[:]),
                             reads=[Bsb1], writes=[Bsb])
                    for r in range(2):
                        for tih in range(2):
                            dma("sync", vs[:, r, tih * 8:(tih + 1) * 8, :],
                                v_all(tih, r)[:, 768:1024].rearrange("(pl p) c -> p pl c", p=128), [], [Bvs], Bvs)
                    kq = 0
                    pend = []

                    def s_stage_a(j, qc, kind, kk):
                        hb, kvh = j % 2, j // 3
                        lo = 128 if (qc == 0 and kind == 0) else 0
                        for pi in range(lo // 128, 4):
                            p = qc * 4 + pi
                            g = 2 * p - 1 + kind
                            r, pl_ = g % 2, g // 2
                            P.op("tensor", lambda e, pi=pi, p=p, r=r, pl_=pl_: e.matmul(
                                pS[kk][:, pi * 128:(pi + 1) * 128], lhsT=skT[:, kvh, r, pl_ * 128:(pl_ + 1) * 128],
                                rhs=qT[hb][:, p * 128:(p + 1) * 128], start=True, stop=True),
                                reads=[BskT, BqT[hb]], writes=[BpS[kk]])
                        P.op("vector", lambda e: e.tensor_tensor(
                            out=X[kk][:, lo:512], in0=pS[kk][:, lo:512], in1=sbias[:, kind * 6 + j, lo:512], op=ALU.add),
                            reads=[BpS[kk], Bsb], writes=[BX[kk]])
                        P.op("scalar", lambda e: e.activation(out=PT[kk][:, lo:512], in_=X[kk][:, lo:512], func=AF.Exp),
                             reads=[BX[kk]], writes=[BPT[kk]])

                    def s_stage_b(j, qc, kind, kk):
                        kvh = j // 3
                        ob = (j * 4 + qc) % 2
                        lo = 128 if (qc == 0 and kind == 0) else 0
                        for pi in range(lo // 128, 4):
                            p = qc * 4 + pi
                            g = 2 * p - 1 + kind
                            r, pl_ = g % 2, g // 2
                            P.op("tensor", lambda e, pi=pi, r=r, pl_=pl_: e.matmul(
                                pO[ob][:, pi * 128:(pi + 1) * 128], lhsT=vs[:, r, pl_, kvh * 128:(kvh + 1) * 128],
                                rhs=PT[kk][:, pi * 128:(pi + 1) * 128], start=(kind == 1 and pi == 0), stop=(kind == 0 or (kind == 2 and p == 0)),
                                skip_group_check=True),
                                reads=[Bvs, BPT[kk]], writes=[BpO[ob]])
                        P.op("tensor", lambda e: e.matmul(
                            pL[ob][:, lo:512], lhsT=ones_b[:], rhs=PT[kk][:, lo:512], start=(kind == 1), stop=(kind == 0)),
                            reads=[BPT[kk]], writes=[BpL[ob]])
                        if kind == 0:
                            attn_epilogue(pO[ob][:], pL[ob][:], mixT[:, 6 + j, qc * 512:(qc + 1) * 512], rl[ob][:],
                                          BpO[ob], BpL[ob], Brl[ob], extra=esink[:, j:j + 1])

                    for j in range(6):
                        if j + 1 < 6:
                            hb1 = (j + 1) % 2
                            dma("sync", qT[hb1][:], qT_d[(7 + j) * 128:(8 + j) * 128, :], [], [BqT[hb1]], BqT[hb1])
                        for qc in range(4):
                            for kind in (1, 2, 0):
                                kk = kq % NB
                                kq += 1
                                s_stage_a(j, qc, kind, kk)
                                pend.append((j, qc, kind, kk))
                                if len(pend) > NB - 1:
                                    s_stage_b(*pend.pop(0))
                    while pend:
                        s_stage_b(*pend.pop(0))
                    P.end_phase()

            phase_c2()

            with ExitStack() as s_wo:
                wo = [sb(s_wo, f"wo{i}", [128, NCH, 512], BF16) for i in range(2)]
                xq = [sb(s_wo, f"oxq{i}", [128, NTB, 512], F32) for i in range(2)]
                Bwo, Bxq = P.bufs_n("wo", 2), P.bufs_n("oxq", 2)

                def load_q(q, xeng="sync"):
                    sl = q % 2
                    csl = slice(q * 512, (q + 1) * 512)
                    dma("gpsimd", wo[sl][:], wout_d[q].rearrange("p (c j) -> p c j", c=NCH), [], [Bwo[sl]], Bwo[sl])
                    dma(xeng, xq[sl][:], x1_d[:, csl].rearrange("(p t) c -> t p c", t=128), [], [Bxq[sl]], Bxq[sl])

                def phase_c3():
                    with ExitStack() as s3:
                        BmkT, Bmv = P.buf("mkT2"), P.buf("mv2")
                        with ExitStack() as s3d:
                            NB = 3
                            qT = [sb(s3d, f"mqT{i}", [128, NTOK], BF16) for i in range(2)]
                            PT = [sb(s3d, f"mPT{i}", [128, 512], BF16) for i in range(NB)]
                            rl = [sb(s3d, f"mrl{i}", [128, 512], F32) for i in range(2)]
                            pS = [ps(s3d, f"mpS{i}", [128, 512], F32) for i in range(NB)]
                            pO = [ps(s3d, f"mpO{i}", [128, 512], F32) for i in range(2)]
                            pL = [ps(s3d, f"mpL{i}", [128, 512], F32) for i in range(2)]
                            BqT, Brl, BpO, BpL = (P.bufs_n("m" + n, 2) for n in ("qT", "rl", "pO", "pL"))
                            BPT, BpS = (P.bufs_n("m" + n, NB) for n in ("PT", "pS"))
                            kq = 0
                            pend = []

                            def m_stage_b(j, qc, mb, kk):
                                ob = (j * 4 + qc) % 2
                                P.op("tensor", lambda e: e.matmul(
                                    pO[ob][:], lhsT=mv[:, mb, j * 128:(j + 1) * 128], rhs=PT[kk][:],
                                    start=(mb == 0), stop=(mb == 1)),
                                    reads=[Bmv, BPT[kk]], writes=[BpO[ob]])
                                P.op("tensor", lambda e: e.matmul(
                                    pL[ob][:], lhsT=ones_b[:], rhs=PT[kk][:], start=(mb == 0), stop=(mb == 1)),
                                    reads=[BPT[kk]], writes=[BpL[ob]])
                                if mb == 1:
                                    attn_epilogue(pO[ob][:], pL[ob][:], mixT[:, 12 + j, qc * 512:(qc + 1) * 512], rl[ob][:],
                                                  BpO[ob], BpL[ob], Brl[ob])

                            dma("sync", qT[0][:], qT_d[12 * 128:13 * 128, :], [], [BqT[0]], BqT[0])
                            for j in range(4):
                                hb = j % 2
                                if j + 1 < 4:
                                    hb1 = (j + 1) % 2
                                    dma("sync", qT[hb1][:], qT_d[(13 + j) * 128:(14 + j) * 128, :], [], [BqT[hb1]], BqT[hb1])
                                for qc in range(4):
                                    for mb in range(2):
                                        kk = kq % NB
                                        kq += 1
                                        P.op("tensor", lambda e, kk=kk, hb=hb, j=j, mb=mb, qc=qc: e.matmul(
                                            pS[kk][:], lhsT=mkT[:, j, mb * 128:(mb + 1) * 128], rhs=qT[hb][:, qc * 512:(qc + 1) * 512],
                                            start=True, stop=True),
                                            reads=[BmkT, BqT[hb]], writes=[BpS[kk]])
                                        P.op("scalar", lambda e, kk=kk: e.activation(out=PT[kk][:], in_=pS[kk][:], func=AF.Exp),
                                             reads=[BpS[kk]], writes=[BPT[kk]])
                                        pend.append((j, qc, mb, kk))
                                        if len(pend) > NB - 1:
                                            m_stage_b(*pend.pop(0))
                            while pend:
                                m_stage_b(*pend.pop(0))
                            P.end_phase()

                load_q(0, xeng="gpsimd")
                phase_c3()

                def phase_c4():
                    with ExitStack() as s4:
                        pp = [ps(s4, f"opp{i}", [128, 512], F32) for i in range(3)]
                        Bpp = P.bufs_n("opp", 3)
                        BmixT = P.buf("mixTr")
                        k = 0
                        for q in range(4):
                            sl = q % 2
                            csl = slice(q * 512, (q + 1) * 512)
                            if q + 1 < 4:
                                load_q(q + 1)
                            for p in range(NTB):
                                kk = k % 3
                                k += 1
                                rsl = slice(p * 128, (p + 1) * 128)
                                for c in range(16):
                                    P.op("tensor", lambda e, c=c, sl=sl, kk=kk, rsl=rsl: e.matmul(
                                        pp[kk][:], lhsT=mixT[:, c, rsl], rhs=wo[sl][:, c, :], start=(c == 0), stop=(c == 15)),
                                        reads=[Bwo[sl], BmixT], writes=[Bpp[kk]])
                                P.op("vector", lambda e, kk=kk, sl=sl, p=p: e.tensor_tensor(
                                    out=xq[sl][:, p, :], in0=pp[kk][:], in1=xq[sl][:, p, :], op=ALU.add),
                                    reads=[Bpp[kk], Bxq[sl]], writes=[Bxq[sl]])
                            dma("sync", x2_d[:, csl].rearrange("(p t) c -> t p c", t=128), xq[sl][:], [Bxq[sl]], [], Bxq[sl])
                        P.end_phase()

                phase_c4()

        for ti in range(NTILE):
            ffn_tile(x2_d, out_d, ti * TT, 3, wgu_d[1], wd_d[1], f"f2t{ti}")
        P.stopped = False
        for en in ENGINES:
            P.op(en, lambda e: e.nop())
        P.replay()
        K.nops = P.nops
    return nc


_NC_CACHE = {}


def _consts(h):
    ident = np.eye(128, dtype=np.float32)
    tri = np.triu(np.ones((128, 128), np.float32))
    ones = np.ones((128, 128), np.float32)
    s_idx = np.arange(128)[:, None]
    t_idx = np.arange(128)[None, :]
    causal = np.where(t_idx >= s_idx, 0.0, NEG).astype(np.float32)
    allmask = np.full((128, 128), NEG, np.float32)
    zero = np.zeros((128, 128), np.float32)
    if h == 0:
        fmask = np.concatenate([causal, allmask], axis=1)
    else:
        fmask = np.concatenate([zero, causal], axis=1)
    slopes = (2.0 ** (-8.0 * np.arange(1, 7) / 6)).astype(np.float32)
    dist_cur = (t_idx - s_idx).astype(np.float32)
    dist_prev = dist_cur + 128.0
    sb = np.zeros((3, 6, 128, 128), np.float32)
    for j in range(6):
        cur = np.where((dist_cur >= 0) & (dist_cur < 128), -slopes[j] * dist_cur, NEG)
        prev = np.where((dist_prev >= 0) & (dist_prev < 128), -slopes[j] * dist_prev, NEG)
        if h == 0:
            sb[0, j], sb[1, j], sb[2, j] = prev, cur, allmask
        else:
            sb[0, j], sb[1, j], sb[2, j] = allmask, prev, cur
    sbias = np.ascontiguousarray(sb.reshape(18, 128, 128).transpose(1, 0, 2)).reshape(128, 18 * 128)
    return ident, tri, ones, fmask.astype(np.float32), sbias.astype(np.float32)


def kernel(x, mem, ffn1_norm, ffn1_gate, ffn1_up, ffn1_down, mix_norm, mem_norm, w_in, forget_bias,
           w_mem_k, w_mem_v, fox_q_gain, fox_k_gain, swa_q_gain, swa_k_gain, swa_sinks, mem_q_gain,
           mem_k_gain, w_out, ffn2_norm, ffn2_gate, ffn2_up, ffn2_down):
    f32 = np.float32
    A = lambda a: np.ascontiguousarray(np.asarray(a, dtype=f32))
    x = A(x); mem = A(mem)
    B = x.shape[0]

    def lay_gu(g, u):
        g = A(g)[0].reshape(NCH, 128, NFB, 128).transpose(2, 1, 0, 3)
        u = A(u)[0].reshape(NCH, 128, NFB, 128).transpose(2, 1, 0, 3)
        return np.ascontiguousarray(np.stack([g, u], axis=2)).reshape(NFB, 128, 2 * NCH * 128)

    def lay_d(d):
        d = A(d)[0].reshape(NFB, 128, NCH, 128).transpose(2, 1, 0, 3)
        return np.ascontiguousarray(d).reshape(NCH, 128, NFB * 128)

    def lay_cols(w, ncol):
        g = ncol // 512
        w = w.reshape(NCH, 128, g, 512).transpose(2, 1, 0, 3)
        return np.ascontiguousarray(w).reshape(g, 128, NCH * 512)

    wgu1, wgu2 = lay_gu(ffn1_gate, ffn1_up), lay_gu(ffn2_gate, ffn2_up)
    wd1, wd2 = lay_d(ffn1_down), lay_d(ffn2_down)
    wi = A(w_in)[0]
    cuts = np.cumsum([768, 768, 768, 6, 768, 256, 256, 512])
    fq, fk, fv, fl, sq, sk, sv, mq = np.split(wi, cuts[:-1], axis=1)
    win = lay_cols(np.concatenate([fq, fk, sq, sk, mq, fv, sv], axis=1), 4096)
    wfl = np.ascontiguousarray(fl.reshape(NCH, 128, 6).transpose(1, 0, 2)).reshape(128, NCH * 6)
    wmem = np.concatenate([lay_cols(A(w_mem_k)[0], 512), lay_cols(A(w_mem_v)[0], 512)], axis=0)
    wout = lay_cols(A(w_out)[0], 2048)
    grows = np.ascontiguousarray(np.broadcast_to(
        np.stack([A(v)[0] for v in (ffn1_norm, mix_norm, mem_norm, ffn2_norm)])[:, None, :], (4, 128, D)))
    hgains = np.ascontiguousarray(np.stack([A(v)[0] for v in (fox_q_gain, fox_k_gain, swa_q_gain, swa_k_gain,
                                                               mem_q_gain, mem_k_gain)], axis=1))
    fbias = np.ascontiguousarray(np.broadcast_to(np.tile(A(forget_bias)[0], TILE_TB)[None, :], (128, TILE_TB * 6)))
    sinks = np.ascontiguousarray(np.broadcast_to(A(swa_sinks)[0][None, :], (128, 6)))

    if "nc" not in _NC_CACHE:
        import os
        _st = os.environ.get("MK_STOP")
        _NC_CACHE["nc"] = build_program(int(_st) if _st else None)
    nc = _NC_CACHE["nc"]

    in_maps = []
    for c in range(8):
        b, h = c // 2, c % 2
        xb = x[b].reshape(16, 2, 128, D)[:, h].reshape(NTOK, D)
        ident, tri, ones, fmask, sbias = _consts(h)
        in_maps.append({
            "x_loc": np.ascontiguousarray(xb), "mem_b": mem[b], "grows": grows, "hgains": hgains, "fbias": fbias,
            "sinks": sinks, "hsel": np.full((128, 1), float(h), f32), "ident": ident, "tri": tri, "ones": ones,
            "fmask": fmask, "sbias": sbias, "wgu1": wgu1, "wgu2": wgu2, "wd1": wd1, "wd2": wd2, "win": win,
            "wfl": wfl, "wmem": wmem, "wout": wout,
        })
    res = run_bass_kernel_spmd(nc, in_maps, core_ids=list(range(8)))
    out = np.empty((B, SEQ, D), f32)
    for c in range(8):
        b, h = c // 2, c % 2
        out[b].reshape(16, 2, 128, D)[:, h] = np.asarray(res.results[c]["out_loc"]).reshape(16, 128, D)
    return out
```
